# Optimizing a Trainium2 kernel written in Bass

```python
import math
import jax, jax.numpy as jnp
from jax import lax
import numpy as np

D_MODEL = 1024
BATCH = 1
SEQ = 16384
DEPTH = 1
DEC_BATCH = 16
DEC_SEQ = 16
PAST_LEN = 2048

CHUNK = 64
N_HEADS_A = 8
HEAD_DIM = 64
V_DIM = 2 * HEAD_DIM
D_A = N_HEADS_A * V_DIM
QK_WIDTH = N_HEADS_A * 2 * HEAD_DIM
ROPE_DIM = HEAD_DIM // 4
ROPE_THETA = 500000.0
Q_BLOCK = 128
GMLP_CHUNK = 128
N_GROUPS_B = 4
D_B = 1024
GROUP_DIM = D_B // N_GROUPS_B
D_FF = 2816
IN_COLS = 3 * QK_WIDTH + 2 * D_B + 2 * D_MODEL
EPS = 1e-6
SCALE = HEAD_DIM ** -0.5
NEG_INF = -1e30

kernel_name = "hybrid_diffattn_gmlp_streaming_step"


def rmsnorm(x, g):
    x32 = x.astype(jnp.float32)
    y = x32 * lax.rsqrt(jnp.mean(x32 * x32, axis=-1, keepdims=True) + EPS)
    return (y * g.astype(jnp.float32)).astype(x.dtype)


def swiglu(x, w_in, w_out):
    a, b = jnp.split(x @ w_in, 2, axis=-1)
    return (jax.nn.silu(a) * b) @ w_out


def partial_rope(x, pos):
    inv_freq = ROPE_THETA ** (-jnp.arange(0, ROPE_DIM, 2, dtype=jnp.float32) / ROPE_DIM)
    ang = pos.astype(jnp.float32)[:, None] * inv_freq[None, :]
    cos = jnp.cos(ang)[None, :, None, None, :].astype(x.dtype)
    sin = jnp.sin(ang)[None, :, None, None, :].astype(x.dtype)
    half = ROPE_DIM // 2
    x1, x2, xp = x[..., :half], x[..., half:ROPE_DIM], x[..., ROPE_DIM:]
    return jnp.concatenate([x1 * cos - x2 * sin, x2 * cos + x1 * sin, xp], axis=-1)


def diff_attn_core(q, k, v, lam, mask):
    s = jnp.einsum('bqhid,bkhid->bihqk', q, k).astype(jnp.float32) * SCALE
    if mask is not None:
        s = jnp.where(mask, s, NEG_INF)
    a = jax.nn.softmax(s, axis=-1)
    a = a[:, 0] - lam * a[:, 1]
    return jnp.einsum('bhqk,bkhv->bqhv', a.astype(v.dtype), v)


def prompt_diff_attn(q, k, v, lam):
    B, S = q.shape[0], q.shape[1]
    nb = S // Q_BLOCK
    qb = jnp.moveaxis(q.reshape(B, nb, Q_BLOCK, N_HEADS_A, 2, HEAD_DIM), 1, 0)
    k_chunk = jnp.arange(S) // CHUNK

    def one_block(args):
        i, qi = args
        q_chunk = (i * Q_BLOCK + jnp.arange(Q_BLOCK)) // CHUNK
        mask = k_chunk[None, :] <= q_chunk[:, None]
        return diff_attn_core(qi, k, v, lam, mask)

    o = lax.map(one_block, (jnp.arange(nb), qb))
    return jnp.moveaxis(o, 0, 1).reshape(B, S, N_HEADS_A, V_DIM)


def spatial_mix(vn, w_s, b_s):
    B, S, _ = vn.shape
    L = min(S, GMLP_CHUNK)
    nc = S // L
    w = (w_s * jnp.tril(jnp.ones((GMLP_CHUNK, GMLP_CHUNK), w_s.dtype)))[:, :L, :L]
    vr = vn.reshape(B, nc, L, N_GROUPS_B, GROUP_DIM)
    out = jnp.einsum('gij,bnjgc->bnigc', w, vr) + b_s[:, :L].T[None, None, :, :, None]
    return out.reshape(B, S, D_B)


def layer(x, pos, cache_k, cache_v, p, lam_init):
    B, S, _ = x.shape
    h = x + 0.5 * rmsnorm(swiglu(rmsnorm(x, p['ln_ffn1_pre']), p['w_ffn1_in'], p['w_ffn1_out']), p['ln_ffn1_post'])
    n = rmsnorm(h, p['ln_mix_pre'])
    proj = n @ p['w_in']
    splits = np.cumsum([QK_WIDTH, QK_WIDTH, D_A, D_B, D_B, D_MODEL]).tolist()
    q, k, v, u, vg, ga, gb = jnp.split(proj, splits, axis=-1)
    q = partial_rope(q.reshape(B, S, N_HEADS_A, 2, HEAD_DIM), pos)
    k = partial_rope(k.reshape(B, S, N_HEADS_A, 2, HEAD_DIM), pos)
    v = v.reshape(B, S, N_HEADS_A, V_DIM)
    k_rows = k.reshape(B, S, N_HEADS_A, 2 * HEAD_DIM)
    lam = (jnp.exp(jnp.sum(p['lambda_q1'].astype(jnp.float32) * p['lambda_k1'].astype(jnp.float32)))
           - jnp.exp(jnp.sum(p['lambda_q2'].astype(jnp.float32) * p['lambda_k2'].astype(jnp.float32)))
           + lam_init)
    if cache_k is None:
        attn = prompt_diff_attn(q, k, v, lam)
    else:
        P = cache_k.shape[1]
        k_all = jnp.concatenate([cache_k.reshape(B, P, N_HEADS_A, 2, HEAD_DIM), k], axis=1)
        v_all = jnp.concatenate([cache_v, v], axis=1)
        attn = diff_attn_core(q, k_all, v_all, lam, None)
    attn = (rmsnorm(attn, p['ln_subln']) * (1.0 - lam_init)).reshape(B, S, D_A)
    u = jax.nn.gelu(u)
    vn = rmsnorm(jax.nn.gelu(vg), p['ln_sgu'])
    sgu = u * spatial_mix(vn, p['w_spatial'], p['b_spatial'])
    merged = jax.nn.sigmoid(ga) * (attn @ p['w_proj_a']) + jax.nn.sigmoid(gb) * (sgu @ p['w_proj_b'])
    h = h + rmsnorm(merged @ p['w_out'], p['ln_mix_post'])
    y = h + 0.5 * rmsnorm(swiglu(rmsnorm(h, p['ln_ffn2_pre']), p['w_ffn2_in'], p['w_ffn2_out']), p['ln_ffn2_post'])
    return y, k_rows, v, vn


def setup_inputs(seed: int = 0) -> dict:
    key = jax.random.key(seed)
    ks = jax.random.split(key, 32)

    def nrm(k, shape, scale):
        return jax.random.normal(k, shape, jnp.float32) * scale

    def gain(k, dim):
        return 1.0 + 0.05 * jax.random.normal(k, (DEPTH, dim), jnp.float32)

    return {
        'x_prompt': nrm(ks[0], (BATCH, SEQ, D_MODEL), 1.0),
        'x_sample': nrm(ks[1], (DEC_BATCH, DEC_SEQ, D_MODEL), 1.0),
        'cache_k': nrm(ks[2], (DEPTH, DEC_BATCH, PAST_LEN, N_HEADS_A, 2 * HEAD_DIM), 1.0),
        'cache_v': nrm(ks[3], (DEPTH, DEC_BATCH, PAST_LEN, N_HEADS_A, V_DIM), 1.0),
        'ln_ffn1_pre': gain(ks[4], D_MODEL),
        'w_ffn1_in': nrm(ks[5], (DEPTH, D_MODEL, 2 * D_FF), D_MODEL ** -0.5),
        'w_ffn1_out': nrm(ks[6], (DEPTH, D_FF, D_MODEL), D_FF ** -0.5),
        'ln_ffn1_post': gain(ks[7], D_MODEL),
        'ln_mix_pre': gain(ks[8], D_MODEL),
        'w_in': nrm(ks[9], (DEPTH, D_MODEL, IN_COLS), D_MODEL ** -0.5),
        'lambda_q1': nrm(ks[10], (DEPTH, HEAD_DIM), 0.1),
        'lambda_k1': nrm(ks[11], (DEPTH, HEAD_DIM), 0.1),
        'lambda_q2': nrm(ks[12], (DEPTH, HEAD_DIM), 0.1),
        'lambda_k2': nrm(ks[13], (DEPTH, HEAD_DIM), 0.1),
        'ln_subln': gain(ks[14], V_DIM),
        'ln_sgu': gain(ks[15], D_B),
        'w_spatial': nrm(ks[16], (DEPTH, N_GROUPS_B, GMLP_CHUNK, GMLP_CHUNK), GMLP_CHUNK ** -0.5),
        'b_spatial': 1.0 + 0.1 * jax.random.normal(ks[17], (DEPTH, N_GROUPS_B, GMLP_CHUNK), jnp.float32),
        'w_proj_a': nrm(ks[18], (DEPTH, D_A, D_MODEL), D_A ** -0.5),
        'w_proj_b': nrm(ks[19], (DEPTH, D_B, D_MODEL), D_B ** -0.5),
        'w_out': nrm(ks[20], (DEPTH, D_MODEL, D_MODEL), D_MODEL ** -0.5),
        'ln_mix_post': gain(ks[21], D_MODEL),
        'ln_ffn2_pre': gain(ks[22], D_MODEL),
        'w_ffn2_in': nrm(ks[23], (DEPTH, D_MODEL, 2 * D_FF), D_MODEL ** -0.5),
        'w_ffn2_out': nrm(ks[24], (DEPTH, D_FF, D_MODEL), D_FF ** -0.5),
        'ln_ffn2_post': gain(ks[25], D_MODEL),
    }


def reference(x_prompt, x_sample, cache_k, cache_v,
              ln_ffn1_pre, w_ffn1_in, w_ffn1_out, ln_ffn1_post,
              ln_mix_pre, w_in, lambda_q1, lambda_k1, lambda_q2, lambda_k2,
              ln_subln, ln_sgu, w_spatial, b_spatial, w_proj_a, w_proj_b, w_out, ln_mix_post,
              ln_ffn2_pre, w_ffn2_in, w_ffn2_out, ln_ffn2_post):
    S = x_prompt.shape[1]
    P = cache_k.shape[2]
    L = x_sample.shape[1]
    pos_prompt = jnp.arange(S)
    pos_sample = P + jnp.arange(L)
    hp, hs = x_prompt, x_sample
    kp_l, vp_l, ks_l, vs_l, gs_l = [], [], [], [], []
    for l in range(DEPTH):
        p = {
            'ln_ffn1_pre': ln_ffn1_pre[l], 'w_ffn1_in': w_ffn1_in[l], 'w_ffn1_out': w_ffn1_out[l],
            'ln_ffn1_post': ln_ffn1_post[l], 'ln_mix_pre': ln_mix_pre[l], 'w_in': w_in[l],
            'lambda_q1': lambda_q1[l], 'lambda_k1': lambda_k1[l],
            'lambda_q2': lambda_q2[l], 'lambda_k2': lambda_k2[l],
            'ln_subln': ln_subln[l], 'ln_sgu': ln_sgu[l], 'w_spatial': w_spatial[l], 'b_spatial': b_spatial[l],
            'w_proj_a': w_proj_a[l], 'w_proj_b': w_proj_b[l], 'w_out': w_out[l], 'ln_mix_post': ln_mix_post[l],
            'ln_ffn2_pre': ln_ffn2_pre[l], 'w_ffn2_in': w_ffn2_in[l], 'w_ffn2_out': w_ffn2_out[l],
            'ln_ffn2_post': ln_ffn2_post[l],
        }
        lam_init = 0.8 - 0.6 * math.exp(-0.3 * l)
        hp, kp, vp, _ = layer(hp, pos_prompt, None, None, p, lam_init)
        hs, ksm, vsm, gsm = layer(hs, pos_sample, cache_k[l], cache_v[l], p, lam_init)
        kp_l.append(kp); vp_l.append(vp); ks_l.append(ksm); vs_l.append(vsm); gs_l.append(gsm)
    new_k_prompt = jnp.stack(kp_l)
    new_v_prompt = jnp.stack(vp_l)
    new_k_sample = jnp.stack(ks_l)
    new_v_sample = jnp.stack(vs_l)
    new_sgu_v_sample = jnp.stack(gs_l)
    return (hp, hs, new_k_prompt, new_v_prompt, new_k_sample, new_v_sample, new_sgu_v_sample)
```

```python
import numpy as np
import ml_dtypes
from contextlib import ExitStack
import concourse.bass as bass
import concourse.mybir as mybir
from concourse.bass_utils import run_bass_kernel_spmd

F32 = mybir.dt.float32
BF16 = mybir.dt.bfloat16
AF = mybir.ActivationFunctionType
ALU = mybir.AluOpType
AX = mybir.AxisListType

NCORE = 8
D = 1024
DFF = 2816
NLOC = 2048
NBLK = 16
EPS = 1e-6
LAM_INIT = 0.2
NEG = -30000.0
import os as _os
STAGE = int(_os.environ.get("KSTAGE", "6"))
KSUB = float(_os.environ.get("KSUB", "99"))
KSUB2 = float(_os.environ.get("KSUB2", "99"))


class Buf:
    def __init__(self, name):
        self.name = name
        self.w = None
        self.r = []
        self.excl = False


def bufs(name, n):
    return [Buf("%s%d" % (name, i)) for i in range(n)]


class Ev:
    __slots__ = ("key", "sem", "val", "eng")

    def __init__(self, eng):
        self.eng = eng
        self.key = None
        self.sem = None
        self.val = None


class Sched:
    ND = 6

    def __init__(self, nc, stack):
        self.nc = nc
        self.dry = False
        self.eng = {"pe": nc.tensor, "act": nc.scalar, "dve": nc.vector, "pool": nc.gpsimd, "sp": nc.sync}
        self.csem = {}
        self.ccnt = {}
        for e in ("pe", "act", "dve", "pool"):
            self.csem[e] = stack.enter_context(nc.semaphore("c_" + e))
            self.ccnt[e] = 0
        self.dsem = {}
        self.dcnt = {}
        self.dev = {}
        self.drr = {}
        for e in ("sp", "pool", "act"):
            self.dsem[e] = [stack.enter_context(nc.semaphore("d_%s%d" % (e, i))) for i in range(self.ND)]
            self.dcnt[e] = [0] * self.ND
            self.dev[e] = [None] * self.ND
            self.drr[e] = 0
        self.waited = {e: {} for e in self.eng}
        self.last = {e: None for e in self.csem}
        self.pending = {e: [] for e in self.csem}
        self.ninst = 0

    def _resolve(self, ev):
        if ev.val is not None:
            return
        e = ev.eng
        ins, lev = self.last[e]
        assert lev.val is None
        self.ccnt[e] += 1
        ins.then_inc(self.csem[e], 1)
        for p in self.pending[e]:
            p.key = e
            p.sem = self.csem[e]
            p.val = self.ccnt[e]
        self.pending[e] = []
        assert ev.val is not None

    def _wait(self, e, ev):
        if ev is None:
            return
        self._resolve(ev)
        if self.waited[e].get(ev.key, 0) >= ev.val:
            return
        self.eng[e].wait_ge(ev.sem, ev.val)
        self.waited[e][ev.key] = ev.val

    def _deps(self, e, reads, writes, skip_same=False):
        deps = []
        for b in reads:
            if b.w is not None:
                deps.append(b.w)
        for b in writes:
            if b.w is not None:
                deps.append(b.w)
            deps.extend(b.r)
        for ev in deps:
            if skip_same and ev.eng == e:
                continue
            self._wait(e, ev)

    def _record(self, ev, reads, writes):
        for b in writes:
            b.w = ev
            b.r = []
        for b in reads:
            if b not in writes:
                b.r.append(ev)
                if len(b.r) > 12:
                    seen = {}
                    for x in b.r:
                        k = x.eng if x.eng != "dma" else x.key
                        seen[k] = x
                    b.r = list(seen.values())

    def op(self, e, fn, reads=(), writes=()):
        if self.dry:
            return None
        ex = [b for b in reads if b.excl]
        if ex:
            reads = [b for b in reads if not b.excl]
            writes = list(writes) + ex
        self._deps(e, reads, writes, skip_same=(e == "pe"))
        ins = fn(self.eng[e])
        self.ninst += 1
        ev = Ev(e)
        self.last[e] = (ins, ev)
        self.pending[e].append(ev)
        self._record(ev, reads, writes)
        return ev

    def dma(self, e, out, in_, reads=(), writes=(), **kw):
        if self.dry:
            return None
        i = self.drr[e]
        self.drr[e] = (i + 1) % self.ND
        self._wait(e, self.dev[e][i])
        self._deps(e, reads, writes)
        ins = self.eng[e].dma_start(out=out, in_=in_, **kw)
        self.ninst += 1
        self.dcnt[e][i] += 16
        ins.then_inc(self.dsem[e][i], 16)
        ev = Ev("dma")
        ev.key = ("d", e, i)
        ev.sem = self.dsem[e][i]
        ev.val = self.dcnt[e][i]
        self.dev[e][i] = ev
        self._record(ev, reads, writes)
        return ev

    def finish(self, obufs):
        for b in obufs:
            self._wait("sp", b.w)
        for e in self.csem:
            if self.last[e] is not None:
                self._wait("sp", self.last[e][1])
        for e in ("sp", "pool", "act"):
            for ev in self.dev[e]:
                self._wait("sp", ev)


class Stream:
    def __init__(self, S, nslots, name, pf):
        self.S = S
        self.n = nslots
        self.b = bufs(name, nslots)
        self.reqs = []
        self.pf = pf
        self.reset()

    def reset(self):
        self.i = 0
        self.issued = 0

    def get(self, loader):
        if self.S.dry:
            self.reqs.append(loader)
            return 0, self.b[0]
        i = self.i
        self.i += 1
        while self.issued < min(len(self.reqs), i + self.pf + 1):
            j = self.issued
            self.reqs[j](j % self.n, self.b[j % self.n])
            self.issued += 1
        return i % self.n, self.b[i % self.n]


def build_nc():
    nc = bass.Bass("TRN2", target_bir_lowering=False)

    def din(name, shape, dt=F32):
        return nc.dram_tensor(name, list(shape), dt, kind="ExternalInput").ap()

    def dout(name, shape, dt=F32):
        return nc.dram_tensor(name, list(shape), dt, kind="ExternalOutput").ap()

    xp = din("xp", [NCORE * NLOC, D])
    xs = din("xs", [2, 16, D])
    ck = din("ck", [2, 2048, D])
    cv = din("cv", [2, 2048, D])
    w_f1i = din("w_f1i", [D, 2 * DFF])
    w_f1o = din("w_f1o", [DFF, D])
    w_in = din("w_in", [D, 7 * D])
    w_pa = din("w_pa", [D, D])
    w_pb = din("w_pb", [D, D])
    w_out = din("w_out", [D, D])
    w_f2i = din("w_f2i", [D, 2 * DFF])
    w_f2o = din("w_f2o", [DFF, D])
    g_pre = {k: din("g_" + k, [128, 8]) for k in ("f1pre", "mixpre", "f2pre")}
    g_post = {k: din("g_" + k, [1, D]) for k in ("f1post", "mixpost", "f2post", "sgu")}
    g_sub = din("g_sub", [1, 128])
    lamv = din("lamv", [1, 256])
    wsp = din("wsp", [4, 128, 128])
    bsp = din("bsp", [4, 128])
    cosE = din("cosE", [NCORE * NLOC + 48, 64])
    sinE = din("sinE", [NCORE * NLOC + 48, 64])
    maskB_d = din("maskB", [128, 8, 128], BF16)
    trilT_d = din("trilT", [128, 128])
    ident_d = din("ident", [128, 128])

    y_p = dout("y_p", [NLOC, D])
    y_s = dout("y_s", [2, 16, D])
    nk_p = dout("nk_p", [NLOC, D])
    nv_p = dout("nv_p", [NLOC, D])
    nk_s = dout("nk_s", [2, 16, D])
    nv_s = dout("nv_s", [2, 16, D])
    ng_s = dout("ng_s", [2, 16, D])

    h_scr = nc.dram_tensor("h_scr", [NLOC + 128, D], F32, kind="Internal").ap()
    kt_in = nc.dram_tensor("kt_in", [D, NLOC], BF16, kind="Internal").ap()
    v_in = nc.dram_tensor("v_in", [NLOC, D], BF16, kind="Internal").ap()
    kt_all = nc.dram_tensor("kt_all", [NCORE * D, NLOC], BF16, kind="Internal").ap()
    v_all = nc.dram_tensor("v_all", [NCORE * NLOC, D], BF16, kind="Internal").ap()
    NP2U = 56
    wbf = nc.dram_tensor("wbf", [NP2U, 128, 4096], BF16, kind="Internal").ap()

    with ExitStack() as st:
        S = Sched(nc, st)

        def T(name, shape, dt):
            return st.enter_context(nc.sbuf_tensor(name, list(shape), dt))

        H = T("H", [128, 4, D], F32)
        F1 = T("F1", [128, 4, D], F32)
        F2 = T("F2", [128, 4, D], F32)
        F3 = T("F3", [128, 4, D], BF16)
        gT = T("gT", [128, 22, 512], BF16)
        tmC = T("tmC", [128, 4, D], BF16)
        fmA = T("fmA", [128, 8, 512], BF16)
        fmB = T("fmB", [128, 8, 512], BF16)
        fmC = T("fmC", [128, 8, 512], BF16)
        NSCR = 3
        scr = [T("scr%d" % i, [128, D], F32) for i in range(NSCR)]
        NW = 3
        wsl = [T("wsl%d" % i, [128, 8, 512], BF16) for i in range(NW)]
        NKV = 2
        ktsl = [T("ktsl%d" % i, [128, 2048], BF16) for i in range(NKV)]
        vsl = [T("vsl%d" % i, [128, 16, 129], BF16) for i in range(NKV)]
        NET = 2
        ETs = [T("ET%d" % i, [128, 2, 512], BF16) for i in range(NET)]
        gpost = {k: T("gp_" + k, [128, D], F32) for k in g_post}
        gpre = {k: T("gq_" + k, [128, 8], F32) for k in g_pre}
        gsub = T("gsub", [128, 128], F32)
        identf = T("identf", [128, 128], F32)
        identb = T("identb", [128, 128], BF16)
        maskB = T("maskB_sb", [128, 8, 128], BF16)
        wspT = T("wspT", [128, 4, 128], BF16)
        wspS = T("wspS", [48, 4, 48], BF16)
        wtmp = T("wtmp", [128, 4, 128], F32)
        trilT = T("trilT_sb", [128, 128], F32)
        bspT = T("bspT", [128, 4], F32)
        bspS = T("bspS", [48, 4], F32)
        cosT = T("cosT", [128, 4, 64], F32)
        sinT = T("sinT", [128, 4, 64], F32)
        rtmp = T("rtmp", [128, 4, 64], F32)
        lamt = T("lamt", [128, 256], F32)
        lamp = T("lamp", [128, 128], F32)
        lams = T("lams", [128, 4], F32)
        neghalf = T("neghalf", [128, 8], F32)
        ssq = T("ssq", [128, 8], F32)
        rstd = T("rstd", [128, 8], F32)
        junk = T("junk", [128, D], BF16)
        zerob = T("zerob", [128, 512], BF16)
        ssum = T("ssum", [128, 8], F32)
        ktS = T("ktS", [128, 8, 48], BF16)
        vSs = [T("vS%d" % i, [128, 8, 129], BF16) for i in range(2)]
        asm = T("asm", [128, 16], F32)
        otmp = T("otmp", [128, 4, 128], F32)
        otmp2 = T("otmp2", [128, 4, 128], F32)

        ps = st.enter_context(nc.psum_tensor("ps", [128, 8, 512], F32))

        BH, BF1, BF2, BF3, BtmC = bufs("H", 4), bufs("F1", 4), bufs("F2", 4), bufs("F3", 4), bufs("tmC", 4)
        BfmA, BfmB, BfmC = bufs("fmA", 4), bufs("fmB", 4), bufs("fmC", 4)
        BgT = bufs("gT", 22)
        Bscr = bufs("scr", NSCR)
        Bps = bufs("ps", 8)
        for b_ in Bps:
            b_.excl = True
        BET = bufs("ET", NET)
        Bconst = Buf("const")
        Brope = Buf("rope")
        Brtmp = Buf("rtmp")
        Bssq, Brstd, Bjunk = Buf("ssq"), Buf("rstd"), Buf("junk")
        BktS, BvS = Buf("ktS"), Buf("vS")
        Basm, Botmp, Botmp2 = Buf("asm"), Buf("otmp"), Buf("otmp2")
        Bhscr = bufs("hscr", 5)
        Bkt, Bvt = bufs("kt", 32), bufs("vt", 32)
        Bout = Buf("out")
        Blam = Buf("lam")

        WS = Stream(S, NW, "wsl", pf=1)
        KS = Stream(S, NKV, "ktsl", pf=1)
        VS = Stream(S, NKV, "vsl", pf=1)

        state = {"acc": 0, "gen": 0, "scr": 0, "et": 0}

        def acc_bank():
            b = state["acc"]
            state["acc"] = (b + 1) % 4
            return b

        def gen_bank():
            b = 4 + state["gen"]
            state["gen"] = (state["gen"] + 1) % 4
            return b

        def gen_pair():
            if state["gen"] % 2:
                state["gen"] = (state["gen"] + 1) % 4
            b = 4 + state["gen"]
            state["gen"] = (state["gen"] + 2) % 4
            return b

        def scr_next():
            i = state["scr"]
            state["scr"] = (i + 1) % NSCR
            return i

        def setup():
            S.dma("sp", identf[:], ident_d, writes=[Bconst])
            S.dma("sp", maskB[:], maskB_d, writes=[Bconst])
            S.dma("sp", trilT[:], trilT_d, writes=[Bconst])
            for k in g_post:
                S.dma("sp", gpost[k][:], g_post[k].partition_broadcast(128), writes=[Bconst])
            for k in g_pre:
                S.dma("sp", gpre[k][:], g_pre[k], writes=[Bconst])
            S.dma("sp", gsub[:], g_sub.partition_broadcast(128), writes=[Bconst])
            S.dma("sp", lamt[:], lamv.partition_broadcast(128), writes=[Blam])
            S.op("dve", lambda e: e.memset(zerob[:], 0.0), writes=[Bconst])
            S.op("dve", lambda e: e.tensor_copy(out=identb[:], in_=identf[:]), reads=[Bconst], writes=[Bconst])
            S.op("dve", lambda e: e.memset(neghalf[:], -0.5), writes=[Bconst])
            for k in ("f1post", "f2post"):
                S.op("dve", lambda e, k=k: e.tensor_scalar(out=gpost[k][:], in0=gpost[k][:], scalar1=0.5, scalar2=None, op0=ALU.mult),
                     reads=[Bconst], writes=[Bconst])
            S.op("dve", lambda e: e.tensor_scalar(out=gsub[:], in0=gsub[:], scalar1=1.0 - LAM_INIT, scalar2=None, op0=ALU.mult),
                 reads=[Bconst], writes=[Bconst])
            S.op("dve", lambda e: e.tensor_tensor(out=lamp[:, 0:64], in0=lamt[:, 0:64], in1=lamt[:, 64:128], op=ALU.mult), reads=[Blam], writes=[Blam])
            S.op("dve", lambda e: e.tensor_tensor(out=lamp[:, 64:128], in0=lamt[:, 128:192], in1=lamt[:, 192:256], op=ALU.mult), reads=[Blam], writes=[Blam])
            S.op("dve", lambda e: e.reduce_sum(out=lams[:, 0:2], in_=lamp[:].rearrange("p (a f) -> p a f", f=64), axis=AX.X), reads=[Blam], writes=[Blam])
            S.op("act", lambda e: e.activation(out=lams[:, 0:2], in_=lams[:, 0:2], func=AF.Exp), reads=[Blam], writes=[Blam])
            S.op("dve", lambda e: e.tensor_tensor(out=lams[:, 2:3], in0=lams[:, 0:1], in1=lams[:, 1:2], op=ALU.subtract), reads=[Blam], writes=[Blam])
            S.op("dve", lambda e: e.tensor_scalar(out=lams[:, 2:3], in0=lams[:, 2:3], scalar1=LAM_INIT, scalar2=None, op0=ALU.add), reads=[Blam], writes=[Blam])
            S.dma("sp", wtmp[:], wsp.rearrange("g i j -> i g j"), writes=[Bconst])
            with nc.allow_non_contiguous_dma(reason="tiny bias transposes"):
                S.dma("sp", bspT[:], bsp.rearrange("g i -> i g"), writes=[Bconst])
                S.op("dve", lambda e: e.memset(bspS[:], 0.0), writes=[Bconst])
                for s0 in (0, 32):
                    S.dma("sp", bspS[s0:s0 + 16, :], bsp[:, 0:16].rearrange("g i -> i g"), writes=[Bconst])
            S.op("dve", lambda e: e.memset(wspS[:], 0.0), writes=[Bconst])
            for g in range(4):
                S.op("dve", lambda e, g=g: e.tensor_tensor(out=wtmp[:, g, :], in0=wtmp[:, g, :], in1=trilT[:], op=ALU.mult), reads=[Bconst], writes=[Bconst])
            for g in range(4):
                S.op("pe", lambda e, g=g: e.transpose(ps[:, 4, g * 128:(g + 1) * 128], wtmp[:, g, :], identf[:]), reads=[Bconst], writes=[Bps[4]])
            S.op("dve", lambda e: e.tensor_copy(out=wspT[:], in_=ps[:, 4, :].rearrange("p (g i) -> p g i", i=128)), reads=[Bps[4]], writes=[Bconst])
            for s0 in (0, 32):
                S.dma("sp", wspS[s0:s0 + 16, :, s0:s0 + 16], wspT[0:16, :, 0:16], reads=[Bconst], writes=[Bconst])
            for i in range(NKV):
                S.op("dve", lambda e, i=i: e.memset(vsl[i][:, :, 128:129], 1.0), writes=[VS.b[i]])
            for i in range(2):
                S.op("dve", lambda e, i=i: e.memset(vSs[i][:], 0.0), writes=[BvS])
                S.op("dve", lambda e, i=i: e.memset(vSs[i][i * 32:i * 32 + 16, :, 128:129], 1.0), writes=[BvS])
            S.op("dve", lambda e: e.memset(ktS[:], 0.0), writes=[BktS])

        def rstd_chain(n, inv_n, src=None):
            src = ssq if src is None else src
            S.op("dve", lambda e: e.tensor_scalar(out=rstd[:, 0:n], in0=src[:, 0:n], scalar1=inv_n, scalar2=EPS, op0=ALU.mult, op1=ALU.add),
                 reads=[Bssq], writes=[Brstd])
            S.op("act", lambda e: e.activation(out=rstd[:, 0:n], in_=rstd[:, 0:n], func=AF.Sqrt), reads=[Brstd], writes=[Brstd])
            S.op("dve", lambda e: e.reciprocal(out=rstd[:, 0:n], in_=rstd[:, 0:n]), reads=[Brstd], writes=[Brstd])

        def transpose_to_fm(src_ap_fn, nt, tb, dst, Bdst, src_bufs, gain=None):
            for half in range(2):
                b = gen_bank()
                pv = ps[:, b, :].rearrange("p (a c) -> p a c", c=128)
                for i in range(4):
                    k = half * 4 + i
                    S.op("pe", lambda e, k=k, i=i: e.transpose(pv[:, i, 0:nt], src_ap_fn(k), identf[0:nt, 0:nt]),
                         reads=src_bufs + [Bconst], writes=[Bps[b]])
                if gain is None:
                    eng = "act" if half == 0 else "dve"
                    if eng == "act":
                        S.op("act", lambda e: e.activation(out=dst[:, half * 4:half * 4 + 4, tb * 128:tb * 128 + nt], in_=pv[:, :, 0:nt], func=AF.Copy),
                             reads=[Bps[b]], writes=[Bdst[tb]])
                    else:
                        S.op("dve", lambda e: e.tensor_copy(out=dst[:, half * 4:half * 4 + 4, tb * 128:tb * 128 + nt], in_=pv[:, :, 0:nt]),
                             reads=[Bps[b]], writes=[Bdst[tb]])
                else:
                    for i in range(4):
                        k = half * 4 + i
                        if i % 2 == 0:
                            S.op("act", lambda e, k=k, i=i: e.activation(out=dst[:, k, tb * 128:tb * 128 + nt], in_=pv[:, i, 0:nt], func=AF.Copy,
                                                                        scale=gain[:, k:k + 1]),
                                 reads=[Bps[b], Bconst], writes=[Bdst[tb]])
                        else:
                            S.op("dve", lambda e, k=k, i=i: e.tensor_scalar(out=dst[:, k, tb * 128:tb * 128 + nt], in0=pv[:, i, 0:nt],
                                                                           scalar1=gain[:, k:k + 1], scalar2=None, op0=ALU.mult),
                                 reads=[Bps[b], Bconst], writes=[Bdst[tb]])

        def prenorm(blocks, gain, dst, Bdst):
            nb = len(blocks)
            for tb, nt in enumerate(blocks):
                S.op("act", lambda e, tb=tb, nt=nt: e.activation(out=junk[0:nt, :], in_=H[0:nt, tb, :], func=AF.Square, accum_out=ssq[0:nt, tb:tb + 1]),
                     reads=[BH[tb]], writes=[Bssq, Bjunk])
            rstd_chain(nb, 1.0 / D)
            for tb, nt in enumerate(blocks):
                si = scr_next()
                S.op("dve", lambda e, tb=tb, nt=nt, si=si: e.tensor_scalar(out=scr[si][0:nt, :], in0=H[0:nt, tb, :], scalar1=rstd[0:nt, tb:tb + 1],
                                                                            scalar2=None, op0=ALU.mult),
                     reads=[BH[tb], Brstd], writes=[Bscr[si]])
                transpose_to_fm(lambda k, si=si, nt=nt: scr[si][0:nt, k * 128:(k + 1) * 128], nt, tb, dst, Bdst, [Bscr[si]], gain=gain)

        p2units = {}
        Bwbf = bufs("wbf", 56)
        in_phase2 = [False]

        def p2_unit_list():
            L = []
            for u in range(6):
                ncol = min(512, DFF - u * 512)
                L.append((w_f1i, 0, 8, u * 512, ncol))
                L.append((w_f1i, 0, 8, DFF + u * 512, ncol))
            for u in range(2):
                for kg, nk in enumerate((8, 8, 6)):
                    L.append((w_f1o, kg * 1024, nk, u * 512, 512))
            for col0 in (1024, 2048):
                for u in range(2):
                    L.append((w_in, 0, 8, col0 + u * 512, 512))
            for col0 in (0, 5120, 3072, 4096, 6144):
                for u in range(2):
                    L.append((w_in, 0, 8, col0 + u * 512, 512))
            for w in (w_pa, w_pb, w_out):
                for u in range(2):
                    L.append((w, 0, 8, u * 512, 512))
            for u in range(6):
                ncol = min(512, DFF - u * 512)
                L.append((w_f2i, 0, 8, u * 512, ncol))
                L.append((w_f2i, 0, 8, DFF + u * 512, ncol))
            for u in range(2):
                for kg, nk in enumerate((8, 8, 6)):
                    L.append((w_f2o, kg * 1024, nk, u * 512, 512))
            return L

        def preconvert():
            for idx, (w, r0, nk, c0, ncol) in enumerate(p2_unit_list()):
                p2units[(id(w), r0, c0)] = idx
                si = idx % NW
                S.dma("pool", wsl[si][:, 0:nk, 0:ncol], w[r0:r0 + nk * 128, c0:c0 + ncol].rearrange("(k p) f -> p k f", p=128), writes=[WS.b[si]])
                S.dma("sp", wbf[idx, :, 0:nk * ncol].rearrange("p (k f) -> p k f", f=ncol), wsl[si][:, 0:nk, 0:ncol], reads=[WS.b[si]], writes=[Bwbf[idx]])

        def wunit(w, r0, nk, c0, ncol):
            if True:
                idx = p2units[(id(w), r0, c0)]

                def loader(si, b):
                    S.dma("sp", wsl[si][:, 0:nk, 0:ncol], wbf[idx, :, 0:nk * ncol].rearrange("p (k f) -> p k f", f=ncol), reads=[Bwbf[idx]], writes=[b])
            else:
                def loader(si, b):
                    S.dma("pool", wsl[si][:, 0:nk, 0:ncol], w[r0:r0 + nk * 128, c0:c0 + ncol].rearrange("(k p) f -> p k f", p=128), writes=[b])
            return WS.get(loader)

        def linear_tm(blocks, specs, evac):
            for u in range(2):
                slots = [wunit(w, 0, 8, c0 + u * 512, 512) for (_, _, w, c0) in specs]
                for tb, nt in enumerate(blocks):
                    banks = []
                    for (xT, BxT, _, _), (si, wb) in zip(specs, slots):
                        b = acc_bank()
                        banks.append(b)
                        for k in range(8):
                            S.op("pe", lambda e, k=k, b=b, si=si, xT=xT, tb=tb, nt=nt: e.matmul(
                                ps[0:nt, b, :], lhsT=xT[:, k, tb * 128:tb * 128 + nt], rhs=wsl[si][:, k, :], start=(k == 0), stop=(k == 7)),
                                 reads=[BxT[tb], wb], writes=[Bps[b]])
                    evac(tb, nt, u, banks)

        def postnorm_add(blocks, gain_bc):
            nb = len(blocks)
            sv = ssq[:, 0:2 * nb].rearrange("p (b u) -> p b u", u=2)
            S.op("dve", lambda e: e.tensor_tensor(out=ssum[:, 0:nb], in0=sv[:, :, 0], in1=sv[:, :, 1], op=ALU.add), reads=[Bssq], writes=[Bssq])
            rstd_chain(nb, 1.0 / D, src=ssum)
            for tb, nt in enumerate(blocks):
                S.op("dve", lambda e, tb=tb, nt=nt: e.scalar_tensor_tensor(out=F1[0:nt, tb, :], in0=F1[0:nt, tb, :], scalar=rstd[0:nt, tb:tb + 1],
                                                                          in1=gain_bc[0:nt, :], op0=ALU.mult, op1=ALU.mult),
                     reads=[BF1[tb], Brstd, Bconst], writes=[BF1[tb]])
                S.op("dve", lambda e, tb=tb, nt=nt: e.tensor_tensor(out=H[0:nt, tb, :], in0=H[0:nt, tb, :], in1=F1[0:nt, tb, :], op=ALU.add),
                     reads=[BH[tb], BF1[tb]], writes=[BH[tb]])

        def evac_raw_ss(tb, nt, u, banks):
            b = banks[0]
            S.op("act", lambda e: e.activation(out=junk[0:nt, 0:512], in_=ps[0:nt, b, :], func=AF.Square, accum_out=ssq[0:nt, tb * 2 + u:tb * 2 + u + 1]),
                 reads=[Bps[b]], writes=[Bssq, Bjunk])
            dstT = H if _os.environ.get("KF1") == "H" else F1
            S.op("dve", lambda e: e.tensor_copy(out=dstT[0:nt, tb, u * 512:(u + 1) * 512], in_=ps[0:nt, b, :]), reads=[Bps[b]], writes=[BF1[tb]])

        def ffn(blocks, w_i, w_o, gain_bc):
            Tn = (len(blocks) - 1) * 128 + blocks[-1]
            for u in range(6):
                ncol = min(512, DFF - u * 512)
                sa, ba = wunit(w_i, 0, 8, u * 512, ncol)
                sb, bb = wunit(w_i, 0, 8, DFF + u * 512, ncol)
                for jj in range(ncol // 128):
                    j = u * 4 + jj
                    pa_, pb_ = gen_bank(), gen_bank()
                    for (pbk, si, wb) in ((pa_, sa, ba), (pb_, sb, bb)):
                        for k in range(8):
                            S.op("pe", lambda e, k=k, pbk=pbk, si=si: e.matmul(ps[:, pbk, 0:Tn], lhsT=wsl[si][:, k, jj * 128:(jj + 1) * 128],
                                                                              rhs=fmA[:, k, 0:Tn], start=(k == 0), stop=(k == 7)),
                                 reads=BfmA[0:len(blocks)] + [wb], writes=[Bps[pbk]])
                    si2 = scr_next()
                    S.op("act", lambda e: e.activation(out=scr[si2][:, 0:Tn], in_=ps[:, pa_, 0:Tn], func=AF.Silu), reads=[Bps[pa_]], writes=[Bscr[si2]])
                    S.op("dve", lambda e: e.tensor_tensor(out=gT[:, j, 0:Tn], in0=scr[si2][:, 0:Tn], in1=ps[:, pb_, 0:Tn], op=ALU.mult),
                         reads=[Bscr[si2], Bps[pb_]], writes=[BgT[j]])
            if KSUB <= 2:
                return
            for u in range(2):
                bks = [acc_bank() for _ in blocks]
                for kg, nk in enumerate((8, 8, 6)):
                    si, wb = wunit(w_o, kg * 1024, nk, u * 512, 512)
                    for tb, nt in enumerate(blocks):
                        b = bks[tb]
                        for k in range(nk):
                            j = kg * 8 + k
                            S.op("pe", lambda e, k=k, j=j, b=b, si=si, tb=tb, nt=nt: e.matmul(
                                ps[0:nt, b, :], lhsT=gT[:, j, tb * 128:tb * 128 + nt], rhs=wsl[si][:, k, :], start=(j == 0), stop=(j == 21)),
                                 reads=[BgT[j], wb], writes=[Bps[b]])
                if KSUB <= 2.3:
                    continue
                for tb, nt in enumerate(blocks):
                    evac_raw_ss(tb, nt, u, [bks[tb]])
            if KSUB <= 2.6:
                return
            postnorm_add(blocks, gain_bc)

        def rope_inplace(si, nt, tb):
            v = scr[si][0:nt, 0:512].rearrange("p (g d) -> p g d", d=64)
            c = cosT[0:nt, tb, :].rearrange("p (g d) -> p g d", d=8)
            s_ = sinT[0:nt, tb, :].rearrange("p (g d) -> p g d", d=8)
            t = [rtmp[0:nt, i, :].rearrange("p (g d) -> p g d", d=8) for i in range(4)]
            x1, x2 = v[:, :, 0:8], v[:, :, 8:16]
            R = [Bscr[si], Brope]
            S.op("dve", lambda e: e.tensor_tensor(out=t[0], in0=x1, in1=c, op=ALU.mult), reads=R, writes=[Brtmp])
            S.op("dve", lambda e: e.tensor_tensor(out=t[1], in0=x2, in1=s_, op=ALU.mult), reads=R, writes=[Brtmp])
            S.op("dve", lambda e: e.tensor_tensor(out=t[2], in0=x2, in1=c, op=ALU.mult), reads=R, writes=[Brtmp])
            S.op("dve", lambda e: e.tensor_tensor(out=t[3], in0=x1, in1=s_, op=ALU.mult), reads=R, writes=[Brtmp])
            S.op("dve", lambda e: e.tensor_tensor(out=x1, in0=t[0], in1=t[1], op=ALU.subtract), reads=[Brtmp], writes=[Bscr[si]])
            S.op("dve", lambda e: e.tensor_tensor(out=x2, in0=t[2], in1=t[3], op=ALU.add), reads=[Brtmp], writes=[Bscr[si]])

        def load_rope(row0, nrows, nblk):
            if nblk > 1:
                S.dma("sp", cosT[:, 0:nblk, :], cosE[row0:row0 + nblk * 128, :].rearrange("(b p) f -> p b f", p=128), writes=[Brope])
                S.dma("sp", sinT[:, 0:nblk, :], sinE[row0:row0 + nblk * 128, :].rearrange("(b p) f -> p b f", p=128), writes=[Brope])
            else:
                S.dma("sp", cosT[0:nrows, 0, :], cosE[row0:row0 + nrows, :], writes=[Brope])
                S.dma("sp", sinT[0:nrows, 0, :], sinE[row0:row0 + nrows, :], writes=[Brope])

        def proj_rope_T(blocks, col0, dst, Bdst, out_fn=None, dst2=None, Bdst2=None):
            def evac(tb, nt, u, banks):
                b = banks[0]
                si = scr_next()
                S.op("act", lambda e: e.activation(out=scr[si][0:nt, 0:512], in_=ps[0:nt, b, :], func=AF.Copy), reads=[Bps[b]], writes=[Bscr[si]])
                rope_inplace(si, nt, tb)
                if out_fn is not None:
                    out_fn(si, tb, nt, u)
                bb = gen_bank()
                pv = ps[:, bb, :].rearrange("p (a c) -> p a c", c=128)
                for i in range(4):
                    S.op("pe", lambda e, i=i: e.transpose(pv[:, i, 0:nt], scr[si][0:nt, i * 128:(i + 1) * 128], identf[0:nt, 0:nt]),
                         reads=[Bscr[si], Bconst], writes=[Bps[bb]])
                if dst2 is None:
                    S.op("dve", lambda e: e.tensor_copy(out=dst[:, u * 4:u * 4 + 4, tb * 128:tb * 128 + nt], in_=pv[:, :, 0:nt]),
                         reads=[Bps[bb]], writes=[Bdst[tb]])
                else:
                    S.op("dve", lambda e: e.tensor_copy(out=dst[0:64, u * 4:u * 4 + 4, tb * 128:tb * 128 + nt], in_=pv[0:64, :, 0:nt]),
                         reads=[Bps[bb]], writes=[Bdst[tb]])
                    S.op("dve", lambda e: e.tensor_copy(out=dst2[64:128, u * 4:u * 4 + 4, tb * 128:tb * 128 + nt], in_=pv[64:128, :, 0:nt]),
                         reads=[Bps[bb]], writes=[Bdst2[tb]])
            linear_tm(blocks, [(fmA, BfmA, w_in, col0)], evac)

        def phase1(t):
            in_phase2[0] = False
            sample = (t == 32)
            own = (t < 4)
            slot, tt = t // 4, t % 4
            blocks = [48] if sample else [128] * 4
            nb = len(blocks)
            if sample:
                S.op("dve", lambda e: e.memset(H[0:48, 0, :], 0.0), writes=[BH[0]])
                for s0 in range(2):
                    S.dma("sp", H[s0 * 32:s0 * 32 + 16, 0, :], xs[s0], writes=[BH[0]])
                load_rope(NCORE * NLOC, 48, 1)
            else:
                S.dma("sp", H[:, :, :], xp[t * 512:(t + 1) * 512, :].rearrange("(b p) f -> p b f", p=128), writes=BH)
                load_rope(t * 512, 128, 4)
            prenorm(blocks, gpre["f1pre"], fmA, BfmA)
            if KSUB <= 1:
                return
            ffn(blocks, w_f1i, w_f1o, gpost["f1post"])
            if KSUB <= 3:
                return
            if sample:
                S.dma("sp", h_scr[NLOC:NLOC + 48, :], H[0:48, 0, :], reads=[BH[0]], writes=[Bhscr[4]])
            elif own:
                S.dma("sp", h_scr[t * 512:(t + 1) * 512, :].rearrange("(b p) f -> p b f", p=128), H[:, :, :], reads=BH, writes=[Bhscr[t]])
            prenorm(blocks, gpre["mixpre"], fmA, BfmA)

            def k_out(si, tb, nt, u):
                if sample:
                    for s0 in range(2):
                        S.dma("sp", nk_s[s0, :, u * 512:(u + 1) * 512], scr[si][s0 * 32:s0 * 32 + 16, 0:512], reads=[Bscr[si]], writes=[])
                elif own:
                    r0 = t * 512 + tb * 128
                    S.dma("sp", nk_p[r0:r0 + 128, u * 512:(u + 1) * 512], scr[si][:, 0:512], reads=[Bscr[si]], writes=[])
            proj_rope_T(blocks, 1024, fmB, BfmB, out_fn=k_out)
            if sample:
                S.op("dve", lambda e: e.tensor_copy(out=ktS[:, :, 0:48], in_=fmB[:, :, 0:48]), reads=[BfmB[0]], writes=[BktS])
            else:
                S.dma("sp", kt_all[slot * D:(slot + 1) * D, tt * 512:(tt + 1) * 512].rearrange("(h d) n -> d h n", d=128), fmB[:, :, :], reads=BfmB, writes=[Bkt[t]])

            def v_evac(tb, nt, u, banks):
                b = banks[0]
                si = scr_next()
                S.op("act", lambda e: e.activation(out=scr[si][0:nt, 0:512], in_=ps[0:nt, b, :], func=AF.Copy), reads=[Bps[b]], writes=[Bscr[si]])
                if sample:
                    for i in range(2):
                        S.op("dve", lambda e, i=i: e.tensor_copy(out=vSs[i][i * 32:i * 32 + 16, u * 4:u * 4 + 4, 0:128],
                                                               in_=ps[i * 32:i * 32 + 16, b, :].rearrange("p (h f) -> p h f", f=128)),
                             reads=[Bps[b]], writes=[BvS])
                    for s0 in range(2):
                        S.dma("sp", nv_s[s0, :, u * 512:(u + 1) * 512], scr[si][s0 * 32:s0 * 32 + 16, 0:512], reads=[Bscr[si]], writes=[])
                else:
                    S.op("dve", lambda e: e.tensor_copy(out=tmC[:, tb, u * 512:(u + 1) * 512], in_=ps[:, b, :]), reads=[Bps[b]], writes=[BtmC[tb]])
                    if own:
                        r0 = t * 512 + tb * 128
                        S.dma("sp", nv_p[r0:r0 + 128, u * 512:(u + 1) * 512], scr[si][:, 0:512], reads=[Bscr[si]], writes=[])
            linear_tm(blocks, [(fmA, BfmA, w_in, 2048)], v_evac)
            if not sample:
                S.dma("sp", v_all[t * 512:(t + 1) * 512, :].rearrange("(b p) f -> p b f", p=128), tmC[:, :, :], reads=BtmC, writes=[Bvt[t]])

        def gather():
            if S.dry:
                return
            for (src, dst, Bs, Bd, nm) in ((kt_in, kt_all, Bktin, Bktall, "k"), (v_in, v_all, Bvin, Bvall, "v")):
                S._deps("pool", [Bs], [Bd])
                cc = nc.gpsimd.collective_compute("AllGather", ALU.bypass, replica_groups=[list(range(NCORE))], ins=[src.opt()], outs=[dst.opt()])
                sem = st.enter_context(nc.semaphore("cc_" + nm))
                cc.then_inc(sem)
                ev = Ev("cc")
                ev.key = "cc" + nm
                ev.sem = sem
                ev.val = 1
                Bd.w = ev
                Bd.r = []

        def attn_finalize(acc_of, rows, qbs, h, dstF, BdstF):
            p0, p1 = rows
            nq = len(qbs)
            for i, qb in enumerate(qbs):
                for s in range(2):
                    b, off = acc_of(s, qb)
                    S.op("dve", lambda e, b=b, off=off, i=i, s=s: e.reciprocal(out=asm[p0:p1, s * 4 + i:s * 4 + i + 1], in_=ps[p0:p1, b, off + 128:off + 129]),
                         reads=[Bps[b]], writes=[Basm])
            S.op("dve", lambda e: e.tensor_scalar(out=asm[p0:p1, 4:4 + nq], in0=asm[p0:p1, 4:4 + nq], scalar1=lams[p0:p1, 2:3], scalar2=None, op0=ALU.mult),
                 reads=[Basm, Blam], writes=[Basm])
            for i, qb in enumerate(qbs):
                b0, o0 = acc_of(0, qb)
                b1, o1 = acc_of(1, qb)
                S.op("dve", lambda e, i=i, b1=b1, o1=o1: e.tensor_scalar(out=otmp2[p0:p1, i, :], in0=ps[p0:p1, b1, o1:o1 + 128], scalar1=asm[p0:p1, 4 + i:5 + i],
                                                                         scalar2=None, op0=ALU.mult),
                     reads=[Bps[b1], Basm], writes=[Botmp2])
                S.op("dve", lambda e, i=i, b0=b0, o0=o0: e.scalar_tensor_tensor(out=otmp[p0:p1, i, :], in0=ps[p0:p1, b0, o0:o0 + 128], scalar=asm[p0:p1, i:i + 1],
                                                                                in1=otmp2[p0:p1, i, :], op0=ALU.mult, op1=ALU.subtract),
                     reads=[Bps[b0], Basm, Botmp2], writes=[Botmp])
                S.op("act", lambda e, i=i: e.activation(out=otmp2[p0:p1, i, :], in_=otmp[p0:p1, i, :], func=AF.Square, accum_out=ssq[p0:p1, i:i + 1]),
                     reads=[Botmp], writes=[Botmp2, Bssq])
            rstd_chain(nq, 1.0 / 128)
            for i, qb in enumerate(qbs):
                S.op("dve", lambda e, i=i, qb=qb: e.scalar_tensor_tensor(out=dstF[p0:p1, qb, h * 128:(h + 1) * 128], in0=otmp[p0:p1, i, :],
                                                                        scalar=rstd[p0:p1, i:i + 1], in1=gsub[p0:p1, :], op0=ALU.mult, op1=ALU.mult),
                     reads=[Botmp, Brstd, Bconst], writes=[BdstF[qb]])

        def zero_acc(rows, banks_cols):
            for b, ncol in banks_cols:
                S.op("pe", lambda e, b=b, ncol=ncol: e.matmul(ps[0:rows, b, 0:ncol], lhsT=zerob[:, 0:rows], rhs=zerob[:, 0:ncol], start=True, stop=False,
                                                             skip_group_check=True), reads=[Bconst], writes=[Bps[b]])

        def attention_prompt(g):
            nb = 4 * g + 4
            nk = nb * 128

            def acc_of(s, qb):
                i = s * 4 + qb
                return i // 3, (i % 3) * 129

            for h in range(8):
                zero_acc(128, [(0, 387), (1, 387), (2, 258)])
                tiles = [(cp, rp) for cp in range(NCORE) for rp in range(nb)]
                kslot, vslot = {}, {}

                def emit_qk(idx):
                    cp, rp = tiles[idx]
                    if rp == 0:
                        def kl(si, b, cp=cp, h=h):
                            S.dma("sp", ktsl[si][:, 0:nk], kt_all[cp * D + h * 128:cp * D + (h + 1) * 128, 0:nk], reads=Bkt, writes=[b])
                        kslot[cp] = KS.get(kl)
                    ki, kb = kslot[cp]
                    a = rp - 4 * g
                    q0 = max(a, 0) * 128
                    bp = gen_pair()
                    for s in range(2):
                        b = bp + s
                        kT = ktsl[ki][:, rp * 128:(rp + 1) * 128]
                        qs = (fmB if s == 0 else fmC)[:, h, :]
                        if a >= 0:
                            S.op("pe", lambda e: e.matmul(ps[:, b, q0:q0 + 128], lhsT=kT, rhs=qs[:, q0:q0 + 128], start=True, stop=False),
                                 reads=[kb] + BfmB + BfmC, writes=[Bps[b]])
                            S.op("pe", lambda e: e.matmul(ps[:, b, q0:q0 + 128], lhsT=identb[:], rhs=maskB[:, cp, :], start=False, stop=True),
                                 reads=[Bconst], writes=[Bps[b]])
                            if q0 + 128 < 512:
                                S.op("pe", lambda e: e.matmul(ps[:, b, q0 + 128:512], lhsT=kT, rhs=qs[:, q0 + 128:512], start=True, stop=True),
                                     reads=[kb] + BfmB + BfmC, writes=[Bps[b]])
                        else:
                            S.op("pe", lambda e: e.matmul(ps[:, b, 0:512], lhsT=kT, rhs=qs[:, 0:512], start=True, stop=True),
                                 reads=[kb] + BfmB + BfmC, writes=[Bps[b]])
                    return bp, q0, a

                def emit_exp_av(idx, bp, q0, a):
                    cp, rp = tiles[idx]
                    ei = state["et"]
                    state["et"] = (ei + 1) % NET
                    for s in range(2):
                        S.op("act", lambda e: e.activation(out=ETs[ei][:, s, q0:512], in_=ps[:, bp + s, q0:512], func=AF.Exp, scale=0.125),
                             reads=[Bps[bp + s]], writes=[BET[ei]])
                    if rp == 0:
                        def vl(si, b, cp=cp, h=h):
                            S.dma("sp", vsl[si][:, 0:nb, 0:128],
                                  v_all[cp * NLOC:cp * NLOC + nk, h * 128:(h + 1) * 128].rearrange("(r p) f -> p r f", p=128), reads=Bvt, writes=[b])
                        vslot[cp] = VS.get(vl)
                    vi, vb = vslot[cp]
                    for qb in range(max(a, 0), 4):
                        for s in range(2):
                            ab, off = acc_of(s, qb)
                            S.op("pe", lambda e: e.matmul(ps[:, ab, off:off + 129], lhsT=ETs[ei][:, s, qb * 128:(qb + 1) * 128], rhs=vsl[vi][:, rp, :],
                                                          start=False, stop=(cp == NCORE - 1 and rp == 4 * g + qb), skip_group_check=True),
                                 reads=[BET[ei], vb], writes=[Bps[ab]])

                cur = emit_qk(0)
                for idx in range(len(tiles)):
                    nxt = emit_qk(idx + 1) if idx + 1 < len(tiles) else None
                    emit_exp_av(idx, *cur)
                    cur = nxt
                attn_finalize(acc_of, (0, 128), [0, 1, 2, 3], h, F1, BF1)

        def attention_sample():
            def acc_of(s, st_):
                i = s * 2 + st_
                return i // 3, (i % 3) * 129
            for h in range(8):
                zero_acc(48, [(0, 387), (1, 129)])
                for st_ in range(2):
                    def kl(si, b):
                        pass
                    ki, kb = KS.get(kl)
                    for k8 in range(2):
                        sci = scr_next()
                        kc = scr[sci][:, :].rearrange("p (k f) -> p k f", f=128)
                        S.dma("sp", kc, ck[st_, k8 * 1024:(k8 + 1) * 1024, h * 128:(h + 1) * 128].rearrange("(k p) f -> p k f", p=128), writes=[Bscr[sci]])
                        for q4 in range(2):
                            bb = gen_bank()
                            pv = ps[:, bb, :].rearrange("p (a c) -> p a c", c=128)
                            for i in range(4):
                                S.op("pe", lambda e: e.transpose(pv[:, i, :], kc[:, q4 * 4 + i, :], identf[:]), reads=[Bscr[sci], Bconst], writes=[Bps[bb]])
                            c0 = (k8 * 2 + q4) * 512
                            S.op("dve", lambda e: e.tensor_copy(out=ktsl[ki][:, c0:c0 + 512], in_=ps[:, bb, :]), reads=[Bps[bb]], writes=[kb])

                    def vl(si, b):
                        pass
                    vi, vb = VS.get(vl)
                    for k8 in range(2):
                        sci = scr_next()
                        vc = scr[sci][:, :].rearrange("p (k f) -> p k f", f=128)
                        S.dma("sp", vc, cv[st_, k8 * 1024:(k8 + 1) * 1024, h * 128:(h + 1) * 128].rearrange("(k p) f -> p k f", p=128), writes=[Bscr[sci]])
                        S.op("dve", lambda e: e.tensor_copy(out=vsl[vi][:, k8 * 8:(k8 + 1) * 8, 0:128], in_=vc), reads=[Bscr[sci]], writes=[vb])
                    for rp in range(16):
                        bp = gen_pair()
                        for s in range(2):
                            S.op("pe", lambda e: e.matmul(ps[:, bp + s, 0:48], lhsT=ktsl[ki][:, rp * 128:(rp + 1) * 128],
                                                          rhs=(fmB if s == 0 else fmC)[:, h, 0:48], start=True, stop=True),
                                 reads=[kb, BfmB[0], BfmC[0]], writes=[Bps[bp + s]])
                        ei = state["et"]
                        state["et"] = (ei + 1) % NET
                        for s in range(2):
                            S.op("act", lambda e: e.activation(out=ETs[ei][:, s, 0:48], in_=ps[:, bp + s, 0:48], func=AF.Exp, scale=0.125),
                                 reads=[Bps[bp + s]], writes=[BET[ei]])
                        for s in range(2):
                            ab, off = acc_of(s, st_)
                            S.op("pe", lambda e: e.matmul(ps[0:48, ab, off:off + 129], lhsT=ETs[ei][:, s, 0:48], rhs=vsl[vi][:, rp, :],
                                                          start=False, stop=False, skip_group_check=True),
                                 reads=[BET[ei], vb], writes=[Bps[ab]])
                bp = gen_pair()
                for s in range(2):
                    S.op("pe", lambda e: e.matmul(ps[0:48, bp + s, 0:48], lhsT=ktS[:, h, 0:48], rhs=(fmB if s == 0 else fmC)[:, h, 0:48],
                                                  start=True, stop=True),
                         reads=[BktS, BfmB[0], BfmC[0]], writes=[Bps[bp + s]])
                ei = state["et"]
                state["et"] = (ei + 1) % NET
                for s in range(2):
                    S.op("act", lambda e: e.activation(out=ETs[ei][0:48, s, 0:48], in_=ps[0:48, bp + s, 0:48], func=AF.Exp, scale=0.125),
                         reads=[Bps[bp + s]], writes=[BET[ei]])
                for st_ in range(2):
                    p0 = st_ * 32
                    for s in range(2):
                        ab, off = acc_of(s, st_)
                        S.op("pe", lambda e: e.matmul(ps[0:48, ab, off:off + 129], lhsT=ETs[ei][:, s, 0:48], rhs=vSs[st_][:, h, :],
                                                      start=False, stop=True, skip_group_check=True),
                             reads=[BET[ei], BvS], writes=[Bps[ab]])
                for st_ in range(2):
                    attn_finalize(lambda s, qb, st_=st_: acc_of(s, st_), (st_ * 32, st_ * 32 + 16), [0], h, F1, BF1)

        def phase2(t):
            in_phase2[0] = True
            sample = (t == 4)
            blocks = [48] if sample else [128] * 4
            nb = len(blocks)
            if sample:
                S.dma("sp", H[0:48, 0, :], h_scr[NLOC:NLOC + 48, :], reads=[Bhscr[4]], writes=[BH[0]])
                load_rope(NCORE * NLOC, 48, 1)
                S.op("dve", lambda e: e.memset(F1[0:48, 0, :], 0.0), writes=[BF1[0]])
            else:
                S.dma("sp", H[:, :, :], h_scr[t * 512:(t + 1) * 512, :].rearrange("(b p) f -> p b f", p=128), reads=[Bhscr[t]], writes=BH)
                load_rope(t * 512, 128, 4)
            prenorm(blocks, gpre["mixpre"], fmA, BfmA)
            S.op("dve", lambda e: e.memset(fmB[64:128, :, :], 0.0), writes=BfmB)
            S.op("dve", lambda e: e.memset(fmC[0:64, :, :], 0.0), writes=BfmC)
            proj_rope_T(blocks, 0, fmB, BfmB, dst2=fmC, Bdst2=BfmC)
            if KSUB2 <= 1:
                return
            if sample:
                attention_sample()
            else:
                attention_prompt(t)
            if KSUB2 <= 2:
                return
            for tb, nt in enumerate(blocks):
                transpose_to_fm(lambda k, tb=tb, nt=nt: F1[0:nt, tb, k * 128:(k + 1) * 128], nt, tb, fmC, BfmC, [BF1[tb]])

            def evac_a(tb, nt, u, banks):
                b1, b2 = banks
                si = scr_next()
                S.op("act", lambda e: e.activation(out=scr[si][0:nt, 0:512], in_=ps[0:nt, b2, :], func=AF.Sigmoid), reads=[Bps[b2]], writes=[Bscr[si]])
                S.op("dve", lambda e: e.tensor_tensor(out=F2[0:nt, tb, u * 512:(u + 1) * 512], in0=scr[si][0:nt, 0:512], in1=ps[0:nt, b1, :], op=ALU.mult),
                     reads=[Bscr[si], Bps[b1]], writes=[BF2[tb]])
            linear_tm(blocks, [(fmC, BfmC, w_pa, 0), (fmA, BfmA, w_in, 5120)], evac_a)
            if KSUB2 <= 3:
                return

            def evac_uv(tb, nt, u, banks):
                b1, b2 = banks
                S.op("act", lambda e: e.activation(out=F1[0:nt, tb, u * 512:(u + 1) * 512], in_=ps[0:nt, b1, :], func=AF.Gelu_apprx_tanh),
                     reads=[Bps[b1]], writes=[BF1[tb]])
                S.op("act", lambda e: e.activation(out=F3[0:nt, tb, u * 512:(u + 1) * 512], in_=ps[0:nt, b2, :], func=AF.Gelu_apprx_tanh),
                     reads=[Bps[b2]], writes=[BF3[tb]])
                S.op("act", lambda e: e.activation(out=junk[0:nt, 0:512], in_=F3[0:nt, tb, u * 512:(u + 1) * 512], func=AF.Square,
                                                   accum_out=ssq[0:nt, tb * 2 + u:tb * 2 + u + 1]),
                     reads=[BF3[tb]], writes=[Bssq, Bjunk])
            linear_tm(blocks, [(fmA, BfmA, w_in, 3072), (fmA, BfmA, w_in, 4096)], evac_uv)
            sv = ssq[:, 0:2 * nb].rearrange("p (b u) -> p b u", u=2)
            S.op("dve", lambda e: e.tensor_tensor(out=ssum[:, 0:nb], in0=sv[:, :, 0], in1=sv[:, :, 1], op=ALU.add), reads=[Bssq], writes=[Bssq])
            rstd_chain(nb, 1.0 / D, src=ssum)
            for tb, nt in enumerate(blocks):
                if sample:
                    si = scr_next()
                    S.op("dve", lambda e: e.scalar_tensor_tensor(out=scr[si][0:nt, :], in0=F3[0:nt, tb, :], scalar=rstd[0:nt, tb:tb + 1], in1=gpost["sgu"][0:nt, :],
                                                                 op0=ALU.mult, op1=ALU.mult), reads=[BF3[tb], Brstd, Bconst], writes=[Bscr[si]])
                    for s0 in range(2):
                        S.dma("sp", ng_s[s0, :, :], scr[si][s0 * 32:s0 * 32 + 16, :], reads=[Bscr[si]], writes=[])
                    S.op("dve", lambda e: e.tensor_copy(out=tmC[0:nt, tb, :], in_=scr[si][0:nt, :]), reads=[Bscr[si]], writes=[BtmC[tb]])
                else:
                    S.op("dve", lambda e, tb=tb, nt=nt: e.scalar_tensor_tensor(out=tmC[0:nt, tb, :], in0=F3[0:nt, tb, :], scalar=rstd[0:nt, tb:tb + 1],
                                                                              in1=gpost["sgu"][0:nt, :], op0=ALU.mult, op1=ALU.mult),
                         reads=[BF3[tb], Brstd, Bconst], writes=[BtmC[tb]])
                si = scr_next()
                for half in range(2):
                    b = gen_bank()
                    for gg in range(2):
                        g_ = half * 2 + gg
                        wT = wspS[0:48, g_, 0:48] if sample else wspT[:, g_, :]
                        S.op("pe", lambda e, b=b, gg=gg, g_=g_, wT=wT, tb=tb, nt=nt: e.matmul(ps[0:nt, b, gg * 256:(gg + 1) * 256], lhsT=wT,
                                                                                           rhs=tmC[0:nt, tb, g_ * 256:(g_ + 1) * 256], start=True, stop=True),
                             reads=[BtmC[tb], Bconst], writes=[Bps[b]])
                    for gg in range(2):
                        g_ = half * 2 + gg
                        bs = bspS if sample else bspT
                        S.op("dve", lambda e, b=b, gg=gg, g_=g_, bs=bs, tb=tb, nt=nt, si=si: e.scalar_tensor_tensor(
                            out=scr[si][0:nt, g_ * 256:(g_ + 1) * 256], in0=ps[0:nt, b, gg * 256:(gg + 1) * 256], scalar=bs[0:nt, g_:g_ + 1],
                            in1=F1[0:nt, tb, g_ * 256:(g_ + 1) * 256], op0=ALU.add, op1=ALU.mult),
                             reads=[Bps[b], BF1[tb], Bconst], writes=[Bscr[si]])
                transpose_to_fm(lambda k, si=si, nt=nt: scr[si][0:nt, k * 128:(k + 1) * 128], nt, tb, fmC, BfmC, [Bscr[si]])

            if KSUB2 <= 4:
                return
            def evac_b(tb, nt, u, banks):
                b1, b2 = banks
                si = scr_next()
                S.op("act", lambda e: e.activation(out=scr[si][0:nt, 0:512], in_=ps[0:nt, b2, :], func=AF.Sigmoid), reads=[Bps[b2]], writes=[Bscr[si]])
                S.op("dve", lambda e: e.tensor_tensor(out=scr[si][0:nt, 0:512], in0=scr[si][0:nt, 0:512], in1=ps[0:nt, b1, :], op=ALU.mult),
                     reads=[Bscr[si], Bps[b1]], writes=[Bscr[si]])
                S.op("dve", lambda e: e.tensor_tensor(out=F2[0:nt, tb, u * 512:(u + 1) * 512], in0=F2[0:nt, tb, u * 512:(u + 1) * 512], in1=scr[si][0:nt, 0:512], op=ALU.add),
                     reads=[Bscr[si], BF2[tb]], writes=[BF2[tb]])
            linear_tm(blocks, [(fmC, BfmC, w_pb, 0), (fmA, BfmA, w_in, 6144)], evac_b)
            for tb, nt in enumerate(blocks):
                transpose_to_fm(lambda k, tb=tb, nt=nt: F2[0:nt, tb, k * 128:(k + 1) * 128], nt, tb, fmB, BfmB, [BF2[tb]])
            linear_tm(blocks, [(fmB, BfmB, w_out, 0)], evac_raw_ss)
            postnorm_add(blocks, gpost["mixpost"])
            prenorm(blocks, gpre["f2pre"], fmA, BfmA)
            ffn(blocks, w_f2i, w_f2o, gpost["f2post"])
            if sample:
                for s0 in range(2):
                    S.dma("sp", y_s[s0, :, :], H[s0 * 32:s0 * 32 + 16, 0, :], reads=[BH[0]], writes=[])
            else:
                S.dma("sp", y_p[t * 512:(t + 1) * 512, :].rearrange("(b p) f -> p b f", p=128), H[:, :, :], reads=BH, writes=[])

        def program():
            for t in range(33):
                if STAGE >= 2 or (STAGE == 1 and t == 0):
                    phase1(t)
            for t in (4, 0, 1, 2, 3):
                if STAGE >= 6 or (STAGE == 4 and t == 4) or (STAGE == 5 and t in (4, 0)):
                    phase2(t)

        setup()
        preconvert()
        S.dry = True
        program()
        S.dry = False
        for s_ in (WS, KS, VS):
            s_.reset()
        for k in state:
            state[k] = 0
        program()
        S.finish([Bout])
        print("instructions:", S.ninst, "sbuf remaining:", nc.sbuf_bytes_remaining)
    return nc


_NC = None


def _rope_tables(pos):
    inv_freq = (500000.0 ** (-np.arange(0, 16, 2, dtype=np.float32) / 16)).astype(np.float32)
    ang = pos.astype(np.float32)[:, None] * inv_freq[None, :]
    cos = np.cos(ang).astype(np.float32)
    sin = np.sin(ang).astype(np.float32)
    return np.tile(cos, (1, 8)), np.tile(sin, (1, 8))


def kernel(x_prompt, x_sample, cache_k, cache_v,
           ln_ffn1_pre, w_ffn1_in, w_ffn1_out, ln_ffn1_post,
           ln_mix_pre, w_in, lambda_q1, lambda_k1, lambda_q2, lambda_k2,
           ln_subln, ln_sgu, w_spatial, b_spatial, w_proj_a, w_proj_b, w_out, ln_mix_post,
           ln_ffn2_pre, w_ffn2_in, w_ffn2_out, ln_ffn2_post):
    global _NC
    f = lambda a: np.ascontiguousarray(np.asarray(a, dtype=np.float32))
    if _NC is None:
        _NC = build_nc()
    nc = _NC
    xp = f(x_prompt)[0].reshape(NBLK, NCORE, 128, D)
    xs = f(x_sample)
    ck = f(cache_k)[0].reshape(16, 2048, D)
    cv = f(cache_v)[0].reshape(16, 2048, D)
    fm = lambda g: np.ascontiguousarray(f(g).reshape(8, 128).T)
    common = {
        "w_f1i": f(w_ffn1_in)[0], "w_f1o": f(w_ffn1_out)[0], "w_in": f(w_in)[0], "w_pa": f(w_proj_a)[0], "w_pb": f(w_proj_b)[0],
        "w_out": f(w_out)[0], "w_f2i": f(w_ffn2_in)[0], "w_f2o": f(w_ffn2_out)[0],
        "g_f1pre": fm(ln_ffn1_pre), "g_mixpre": fm(ln_mix_pre), "g_f2pre": fm(ln_ffn2_pre),
        "g_f1post": f(ln_ffn1_post), "g_mixpost": f(ln_mix_post), "g_f2post": f(ln_ffn2_post), "g_sgu": f(ln_sgu),
        "g_sub": f(ln_subln),
        "lamv": np.ascontiguousarray(np.concatenate([f(lambda_q1), f(lambda_k1), f(lambda_q2), f(lambda_k2)], 1).reshape(1, 256)),
        "wsp": f(w_spatial)[0], "bsp": f(b_spatial)[0],
        "ident": np.eye(128, dtype=np.float32),
        "trilT": np.ascontiguousarray(np.tril(np.ones((128, 128), np.float32))),
    }
    in_maps = []
    for c in range(NCORE):
        order = [c] + [r for r in range(NCORE) if r != c]
        pos = np.concatenate([((np.arange(NBLK)[:, None] * NCORE + r) * 128 + np.arange(128)[None, :]).reshape(-1) for r in order])
        spos = np.zeros(48, np.int64)
        spos[0:16] = 2048 + np.arange(16)
        spos[32:48] = 2048 + np.arange(16)
        cosE, sinE = _rope_tables(np.concatenate([pos, spos]))
        mb = np.zeros((128, 8, 128), np.float32)
        for sl in range(8):
            j = order[sl]
            if j == c:
                mb[64:128, sl, 0:64] = NEG
            elif j > c:
                mb[:, sl, :] = NEG
        m = dict(common)
        m.update({
            "xp": np.ascontiguousarray(np.concatenate([xp[:, r].reshape(NLOC, D) for r in order], 0)),
            "xs": np.ascontiguousarray(xs[2 * c:2 * c + 2]),
            "ck": np.ascontiguousarray(ck[2 * c:2 * c + 2]),
            "cv": np.ascontiguousarray(cv[2 * c:2 * c + 2]),
            "cosE": cosE, "sinE": sinE,
            "maskB": mb.astype(ml_dtypes.bfloat16),
        })
        in_maps.append(m)
    res = run_bass_kernel_spmd(nc, in_maps, core_ids=list(range(NCORE)))
    R = res.results

    def gp(name):
        a = np.stack([np.asarray(R[c][name], dtype=np.float32).reshape(NBLK, 128, D) for c in range(NCORE)], axis=1)
        return a.reshape(1, NBLK * NCORE * 128, D)

    def gs(name):
        return np.concatenate([np.asarray(R[c][name], dtype=np.float32) for c in range(NCORE)], axis=0)

    y_prompt = gp("y_p")
    y_sample = gs("y_s")
    nkp = gp("nk_p").reshape(1, 1, 16384, 8, 128)
    nvp = gp("nv_p").reshape(1, 1, 16384, 8, 128)
    nks = gs("nk_s").reshape(1, 16, 16, 8, 128)
    nvs = gs("nv_s").reshape(1, 16, 16, 8, 128)
    ngs = gs("ng_s").reshape(1, 16, 16, D)
    return (y_prompt, y_sample, nkp, nvp, nks, nvs, ngs)
```

```python
import numpy as np
import ml_dtypes
from contextlib import ExitStack
import concourse.bass as bass
import concourse.mybir as mybir
from concourse.bass_utils import run_bass_kernel_spmd

F32 = mybir.dt.float32
BF16 = mybir.dt.bfloat16
AF = mybir.ActivationFunctionType
ALU = mybir.AluOpType
AX = mybir.AxisListType

NCORE = 8
D = 1024
DFF = 2816
NLOC = 2048
NBLK = 16
EPS = 1e-6
LAM_INIT = 0.2
NEG = -30000.0
import os as _os
STAGE = int(_os.environ.get("KSTAGE", "6"))
KSUB = float(_os.environ.get("KSUB", "99"))
KSUB2 = float(_os.environ.get("KSUB2", "99"))


class Buf:
    def __init__(self, name):
        self.name = name
        self.w = None
        self.r = []
        self.excl = False


def bufs(name, n):
    return [Buf("%s%d" % (name, i)) for i in range(n)]


class Ev:
    __slots__ = ("key", "sem", "val", "eng")

    def __init__(self, eng):
        self.eng = eng
        self.key = None
        self.sem = None
        self.val = None


class Sched:
    ND = 6

    def __init__(self, nc, stack):
        self.nc = nc
        self.dry = False
        self.eng = {"pe": nc.tensor, "act": nc.scalar, "dve": nc.vector, "pool": nc.gpsimd, "sp": nc.sync}
        self.csem = {}
        self.ccnt = {}
        for e in ("pe", "act", "dve", "pool"):
            self.csem[e] = stack.enter_context(nc.semaphore("c_" + e))
            self.ccnt[e] = 0
        self.dsem = {}
        self.dcnt = {}
        self.dev = {}
        self.drr = {}
        for e in ("sp", "pool", "act"):
            self.dsem[e] = [stack.enter_context(nc.semaphore("d_%s%d" % (e, i))) for i in range(self.ND)]
            self.dcnt[e] = [0] * self.ND
            self.dev[e] = [None] * self.ND
            self.drr[e] = 0
        self.waited = {e: {} for e in self.eng}
        self.last = {e: None for e in self.csem}
        self.pending = {e: [] for e in self.csem}
        self.ninst = 0

    def _resolve(self, ev):
        if ev.val is not None:
            return
        e = ev.eng
        ins, lev = self.last[e]
        assert lev.val is None
        self.ccnt[e] += 1
        ins.then_inc(self.csem[e], 1)
        for p in self.pending[e]:
            p.key = e
            p.sem = self.csem[e]
            p.val = self.ccnt[e]
        self.pending[e] = []
        assert ev.val is not None

    def _wait(self, e, ev):
        if ev is None:
            return
        self._resolve(ev)
        if self.waited[e].get(ev.key, 0) >= ev.val:
            return
        self.eng[e].wait_ge(ev.sem, ev.val)
        self.waited[e][ev.key] = ev.val

    def _deps(self, e, reads, writes, skip_same=False):
        deps = []
        for b in reads:
            if b.w is not None:
                deps.append(b.w)
        for b in writes:
            if b.w is not None:
                deps.append(b.w)
            deps.extend(b.r)
        for ev in deps:
            if skip_same and ev.eng == e:
                continue
            self._wait(e, ev)

    def _record(self, ev, reads, writes):
        for b in writes:
            b.w = ev
            b.r = []
        for b in reads:
            if b not in writes:
                b.r.append(ev)
                if len(b.r) > 12:
                    seen = {}
                    for x in b.r:
                        k = x.eng if x.eng != "dma" else x.key
                        seen[k] = x
                    b.r = list(seen.values())

    def op(self, e, fn, reads=(), writes=()):
        if self.dry:
            return None
        ex = [b for b in reads if b.excl]
        if ex:
            reads = [b for b in reads if not b.excl]
            writes = list(writes) + ex
        self._deps(e, reads, writes, skip_same=(e == "pe"))
        ins = fn(self.eng[e])
        self.ninst += 1
        ev = Ev(e)
        self.last[e] = (ins, ev)
        self.pending[e].append(ev)
        self._record(ev, reads, writes)
        return ev

    def dma(self, e, out, in_, reads=(), writes=(), **kw):
        if self.dry:
            return None
        i = self.drr[e]
        self.drr[e] = (i + 1) % self.ND
        self._wait(e, self.dev[e][i])
        self._deps(e, reads, writes)
        ins = self.eng[e].dma_start(out=out, in_=in_, **kw)
        self.ninst += 1
        self.dcnt[e][i] += 16
        ins.then_inc(self.dsem[e][i], 16)
        ev = Ev("dma")
        ev.key = ("d", e, i)
        ev.sem = self.dsem[e][i]
        ev.val = self.dcnt[e][i]
        self.dev[e][i] = ev
        self._record(ev, reads, writes)
        return ev

    def finish(self, obufs):
        for b in obufs:
            self._wait("sp", b.w)
        for e in self.csem:
            if self.last[e] is not None:
                self._wait("sp", self.last[e][1])
        for e in ("sp", "pool", "act"):
            for ev in self.dev[e]:
                self._wait("sp", ev)


class Stream:
    def __init__(self, S, nslots, name, pf):
        self.S = S
        self.n = nslots
        self.b = bufs(name, nslots)
        self.reqs = []
        self.pf = pf
        self.reset()

    def reset(self):
        self.i = 0
        self.issued = 0

    def get(self, loader):
        if self.S.dry:
            self.reqs.append(loader)
            return 0, self.b[0]
        i = self.i
        self.i += 1
        while self.issued < min(len(self.reqs), i + self.pf + 1):
            j = self.issued
            self.reqs[j](j % self.n, self.b[j % self.n])
            self.issued += 1
        return i % self.n, self.b[i % self.n]


def build_nc():
    nc = bass.Bass("TRN2", target_bir_lowering=False)

    def din(name, shape, dt=F32):
        return nc.dram_tensor(name, list(shape), dt, kind="ExternalInput").ap()

    def dout(name, shape, dt=F32):
        return nc.dram_tensor(name, list(shape), dt, kind="ExternalOutput").ap()

    xp = din("xp", [NCORE * NLOC, D])
    xs = din("xs", [2, 16, D])
    ck = din("ck", [2, 2048, D])
    cv = din("cv", [2, 2048, D])
    w_f1i = din("w_f1i", [D, 2 * DFF])
    w_f1o = din("w_f1o", [DFF, D])
    w_in = din("w_in", [D, 7 * D])
    w_pa = din("w_pa", [D, D])
    w_pb = din("w_pb", [D, D])
    w_out = din("w_out", [D, D])
    w_f2i = din("w_f2i", [D, 2 * DFF])
    w_f2o = din("w_f2o", [DFF, D])
    g_pre = {k: din("g_" + k, [128, 8]) for k in ("f1pre", "mixpre", "f2pre")}
    g_post = {k: din("g_" + k, [1, D]) for k in ("f1post", "mixpost", "f2post", "sgu")}
    g_sub = din("g_sub", [1, 128])
    lamv = din("lamv", [1, 256])
    wsp = din("wsp", [4, 128, 128])
    bsp = din("bsp", [4, 128])
    cosE = din("cosE", [NCORE * NLOC + 48, 64])
    sinE = din("sinE", [NCORE * NLOC + 48, 64])
    maskB_d = din("maskB", [128, 8, 128], BF16)
    trilT_d = din("trilT", [128, 128])
    ident_d = din("ident", [128, 128])

    y_p = dout("y_p", [NLOC, D])
    y_s = dout("y_s", [2, 16, D])
    nk_p = dout("nk_p", [NLOC, D])
    nv_p = dout("nv_p", [NLOC, D])
    nk_s = dout("nk_s", [2, 16, D])
    nv_s = dout("nv_s", [2, 16, D])
    ng_s = dout("ng_s", [2, 16, D])

    h_scr = nc.dram_tensor("h_scr", [NLOC + 128, D], F32, kind="Internal").ap()
    kt_in = nc.dram_tensor("kt_in", [D, NLOC], BF16, kind="Internal").ap()
    v_in = nc.dram_tensor("v_in", [NLOC, D], BF16, kind="Internal").ap()
    kt_all = nc.dram_tensor("kt_all", [NCORE * D, NLOC], BF16, kind="Internal").ap()
    v_all = nc.dram_tensor("v_all", [NCORE * NLOC, D], BF16, kind="Internal").ap()
    NP2U = 56
    wbf = nc.dram_tensor("wbf", [NP2U, 128, 4096], BF16, kind="Internal").ap()

    with ExitStack() as st:
        S = Sched(nc, st)

        def T(name, shape, dt):
            return st.enter_context(nc.sbuf_tensor(name, list(shape), dt))

        H = T("H", [128, 4, D], F32)
        F1 = T("F1", [128, 4, D], F32)
        F2 = T("F2", [128, 4, D], F32)
        F3 = T("F3", [128, 4, D], BF16)
        gT = T("gT", [128, 22, 512], BF16)
        tmC = T("tmC", [128, 4, D], BF16)
        fmA = T("fmA", [128, 8, 512], BF16)
        fmB = T("fmB", [128, 8, 512], BF16)
        fmC = T("fmC", [128, 8, 512], BF16)
        NSCR = 3
        scr = [T("scr%d" % i, [128, D], F32) for i in range(NSCR)]
        NW = 3
        wsl = [T("wsl%d" % i, [128, 8, 512], BF16) for i in range(NW)]
        NKV = 2
        ktsl = [T("ktsl%d" % i, [128, 2048], BF16) for i in range(NKV)]
        vsl = [T("vsl%d" % i, [128, 16, 129], BF16) for i in range(NKV)]
        NET = 2
        ETs = [T("ET%d" % i, [128, 2, 512], BF16) for i in range(NET)]
        gpost = {k: T("gp_" + k, [128, D], F32) for k in g_post}
        gpre = {k: T("gq_" + k, [128, 8], F32) for k in g_pre}
        gsub = T("gsub", [128, 128], F32)
        identf = T("identf", [128, 128], F32)
        identb = T("identb", [128, 128], BF16)
        maskB = T("maskB_sb", [128, 8, 128], BF16)
        wspT = T("wspT", [128, 4, 128], BF16)
        wspS = T("wspS", [48, 4, 48], BF16)
        wtmp = T("wtmp", [128, 4, 128], F32)
        trilT = T("trilT_sb", [128, 128], F32)
        bspT = T("bspT", [128, 4], F32)
        bspS = T("bspS", [48, 4], F32)
        cosT = T("cosT", [128, 4, 64], F32)
        sinT = T("sinT", [128, 4, 64], F32)
        rtmp = T("rtmp", [128, 4, 64], F32)
        lamt = T("lamt", [128, 256], F32)
        lamp = T("lamp", [128, 128], F32)
        lams = T("lams", [128, 4], F32)
        neghalf = T("neghalf", [128, 8], F32)
        ssq = T("ssq", [128, 8], F32)
        rstd = T("rstd", [128, 8], F32)
        junk = T("junk", [128, D], BF16)
        zerob = T("zerob", [128, 512], BF16)
        ssum = T("ssum", [128, 8], F32)
        ktS = T("ktS", [128, 8, 48], BF16)
        vSs = [T("vS%d" % i, [128, 8, 129], BF16) for i in range(2)]
        asm = T("asm", [128, 16], F32)
        otmp = T("otmp", [128, 4, 128], F32)
        otmp2 = T("otmp2", [128, 4, 128], F32)

        ps = st.enter_context(nc.psum_tensor("ps", [128, 8, 512], F32))

        BH, BF1, BF2, BF3, BtmC = bufs("H", 4), bufs("F1", 4), bufs("F2", 4), bufs("F3", 4), bufs("tmC", 4)
        BfmA, BfmB, BfmC = bufs("fmA", 4), bufs("fmB", 4), bufs("fmC", 4)
        BgT = bufs("gT", 22)
        Bscr = bufs("scr", NSCR)
        Bps = bufs("ps", 8)
        for b_ in Bps:
            b_.excl = True
        BET = bufs("ET", NET)
        Bconst = Buf("const")
        Brope = Buf("rope")
        Brtmp = Buf("rtmp")
        Bssq, Brstd, Bjunk = Buf("ssq"), Buf("rstd"), Buf("junk")
        BktS, BvS = Buf("ktS"), Buf("vS")
        Basm, Botmp, Botmp2 = Buf("asm"), Buf("otmp"), Buf("otmp2")
        Bhscr = bufs("hscr", 5)
        Bkt, Bvt = bufs("kt", 32), bufs("vt", 32)
        Bout = Buf("out")
        Blam = Buf("lam")

        WS = Stream(S, NW, "wsl", pf=1)
        KS = Stream(S, NKV, "ktsl", pf=1)
        VS = Stream(S, NKV, "vsl", pf=1)

        state = {"acc": 0, "gen": 0, "scr": 0, "et": 0}

        def acc_bank():
            b = state["acc"]
            state["acc"] = (b + 1) % 4
            return b

        def gen_bank():
            b = 4 + state["gen"]
            state["gen"] = (state["gen"] + 1) % 4
            return b

        def gen_pair():
            if state["gen"] % 2:
                state["gen"] = (state["gen"] + 1) % 4
            b = 4 + state["gen"]
            state["gen"] = (state["gen"] + 2) % 4
            return b

        def scr_next():
            i = state["scr"]
            state["scr"] = (i + 1) % NSCR
            return i

        def setup():
            S.dma("sp", identf[:], ident_d, writes=[Bconst])
            S.dma("sp", maskB[:], maskB_d, writes=[Bconst])
            S.dma("sp", trilT[:], trilT_d, writes=[Bconst])
            for k in g_post:
                S.dma("sp", gpost[k][:], g_post[k].partition_broadcast(128), writes=[Bconst])
            for k in g_pre:
                S.dma("sp", gpre[k][:], g_pre[k], writes=[Bconst])
            S.dma("sp", gsub[:], g_sub.partition_broadcast(128), writes=[Bconst])
            S.dma("sp", lamt[:], lamv.partition_broadcast(128), writes=[Blam])
            S.op("dve", lambda e: e.memset(zerob[:], 0.0), writes=[Bconst])
            S.op("dve", lambda e: e.tensor_copy(out=identb[:], in_=identf[:]), reads=[Bconst], writes=[Bconst])
            S.op("dve", lambda e: e.memset(neghalf[:], -0.5), writes=[Bconst])
            for k in ("f1post", "f2post"):
                S.op("dve", lambda e, k=k: e.tensor_scalar(out=gpost[k][:], in0=gpost[k][:], scalar1=0.5, scalar2=None, op0=ALU.mult),
                     reads=[Bconst], writes=[Bconst])
            S.op("dve", lambda e: e.tensor_scalar(out=gsub[:], in0=gsub[:], scalar1=1.0 - LAM_INIT, scalar2=None, op0=ALU.mult),
                 reads=[Bconst], writes=[Bconst])
            S.op("dve", lambda e: e.tensor_tensor(out=lamp[:, 0:64], in0=lamt[:, 0:64], in1=lamt[:, 64:128], op=ALU.mult), reads=[Blam], writes=[Blam])
            S.op("dve", lambda e: e.tensor_tensor(out=lamp[:, 64:128], in0=lamt[:, 128:192], in1=lamt[:, 192:256], op=ALU.mult), reads=[Blam], writes=[Blam])
            S.op("dve", lambda e: e.reduce_sum(out=lams[:, 0:2], in_=lamp[:].rearrange("p (a f) -> p a f", f=64), axis=AX.X), reads=[Blam], writes=[Blam])
            S.op("act", lambda e: e.activation(out=lams[:, 0:2], in_=lams[:, 0:2], func=AF.Exp), reads=[Blam], writes=[Blam])
            S.op("dve", lambda e: e.tensor_tensor(out=lams[:, 2:3], in0=lams[:, 0:1], in1=lams[:, 1:2], op=ALU.subtract), reads=[Blam], writes=[Blam])
            S.op("dve", lambda e: e.tensor_scalar(out=lams[:, 2:3], in0=lams[:, 2:3], scalar1=LAM_INIT, scalar2=None, op0=ALU.add), reads=[Blam], writes=[Blam])
            S.dma("sp", wtmp[:], wsp.rearrange("g i j -> i g j"), writes=[Bconst])
            with nc.allow_non_contiguous_dma(reason="tiny bias transposes"):
                S.dma("sp", bspT[:], bsp.rearrange("g i -> i g"), writes=[Bconst])
                S.op("dve", lambda e: e.memset(bspS[:], 0.0), writes=[Bconst])
                for s0 in (0, 32):
                    S.dma("sp", bspS[s0:s0 + 16, :], bsp[:, 0:16].rearrange("g i -> i g"), writes=[Bconst])
            S.op("dve", lambda e: e.memset(wspS[:], 0.0), writes=[Bconst])
            for g in range(4):
                S.op("dve", lambda e, g=g: e.tensor_tensor(out=wtmp[:, g, :], in0=wtmp[:, g, :], in1=trilT[:], op=ALU.mult), reads=[Bconst], writes=[Bconst])
            for g in range(4):
                S.op("pe", lambda e, g=g: e.transpose(ps[:, 4, g * 128:(g + 1) * 128], wtmp[:, g, :], identf[:]), reads=[Bconst], writes=[Bps[4]])
            S.op("dve", lambda e: e.tensor_copy(out=wspT[:], in_=ps[:, 4, :].rearrange("p (g i) -> p g i", i=128)), reads=[Bps[4]], writes=[Bconst])
            for s0 in (0, 32):
                S.dma("sp", wspS[s0:s0 + 16, :, s0:s0 + 16], wspT[0:16, :, 0:16], reads=[Bconst], writes=[Bconst])
            for i in range(NKV):
                S.op("dve", lambda e, i=i: e.memset(vsl[i][:, :, 128:129], 1.0), writes=[VS.b[i]])
            for i in range(2):
                S.op("dve", lambda e, i=i: e.memset(vSs[i][:], 0.0), writes=[BvS])
                S.op("dve", lambda e, i=i: e.memset(vSs[i][i * 32:i * 32 + 16, :, 128:129], 1.0), writes=[BvS])
            S.op("dve", lambda e: e.memset(ktS[:], 0.0), writes=[BktS])

        def rstd_chain(n, inv_n, src=None):
            src = ssq if src is None else src
            S.op("dve", lambda e: e.tensor_scalar(out=rstd[:, 0:n], in0=src[:, 0:n], scalar1=inv_n, scalar2=EPS, op0=ALU.mult, op1=ALU.add),
                 reads=[Bssq], writes=[Brstd])
            S.op("act", lambda e: e.activation(out=rstd[:, 0:n], in_=rstd[:, 0:n], func=AF.Sqrt), reads=[Brstd], writes=[Brstd])
            S.op("dve", lambda e: e.reciprocal(out=rstd[:, 0:n], in_=rstd[:, 0:n]), reads=[Brstd], writes=[Brstd])

        def transpose_to_fm(src_ap_fn, nt, tb, dst, Bdst, src_bufs, gain=None):
            for half in range(2):
                b = gen_bank()
                pv = ps[:, b, :].rearrange("p (a c) -> p a c", c=128)
                for i in range(4):
                    k = half * 4 + i
                    S.op("pe", lambda e, k=k, i=i: e.transpose(pv[:, i, 0:nt], src_ap_fn(k), identf[0:nt, 0:nt]),
                         reads=src_bufs + [Bconst], writes=[Bps[b]])
                if gain is None:
                    eng = "act" if half == 0 else "dve"
                    if eng == "act":
                        S.op("act", lambda e: e.activation(out=dst[:, half * 4:half * 4 + 4, tb * 128:tb * 128 + nt], in_=pv[:, :, 0:nt], func=AF.Copy),
                             reads=[Bps[b]], writes=[Bdst[tb]])
                    else:
                        S.op("dve", lambda e: e.tensor_copy(out=dst[:, half * 4:half * 4 + 4, tb * 128:tb * 128 + nt], in_=pv[:, :, 0:nt]),
                             reads=[Bps[b]], writes=[Bdst[tb]])
                else:
                    for i in range(4):
                        k = half * 4 + i
                        if i % 2 == 0:
                            S.op("act", lambda e, k=k, i=i: e.activation(out=dst[:, k, tb * 128:tb * 128 + nt], in_=pv[:, i, 0:nt], func=AF.Copy,
                                                                        scale=gain[:, k:k + 1]),
                                 reads=[Bps[b], Bconst], writes=[Bdst[tb]])
                        else:
                            S.op("dve", lambda e, k=k, i=i: e.tensor_scalar(out=dst[:, k, tb * 128:tb * 128 + nt], in0=pv[:, i, 0:nt],
                                                                           scalar1=gain[:, k:k + 1], scalar2=None, op0=ALU.mult),
                                 reads=[Bps[b], Bconst], writes=[Bdst[tb]])

        def prenorm(blocks, gain, dst, Bdst):
            nb = len(blocks)
            for tb, nt in enumerate(blocks):
                S.op("act", lambda e, tb=tb, nt=nt: e.activation(out=junk[0:nt, :], in_=H[0:nt, tb, :], func=AF.Square, accum_out=ssq[0:nt, tb:tb + 1]),
                     reads=[BH[tb]], writes=[Bssq, Bjunk])
            rstd_chain(nb, 1.0 / D)
            for tb, nt in enumerate(blocks):
                si = scr_next()
                S.op("dve", lambda e, tb=tb, nt=nt, si=si: e.tensor_scalar(out=scr[si][0:nt, :], in0=H[0:nt, tb, :], scalar1=rstd[0:nt, tb:tb + 1],
                                                                            scalar2=None, op0=ALU.mult),
                     reads=[BH[tb], Brstd], writes=[Bscr[si]])
                transpose_to_fm(lambda k, si=si, nt=nt: scr[si][0:nt, k * 128:(k + 1) * 128], nt, tb, dst, Bdst, [Bscr[si]], gain=gain)

        p2units = {}
        Bwbf = bufs("wbf", 56)
        in_phase2 = [False]

        def p2_unit_list():
            L = []
            for u in range(6):
                ncol = min(512, DFF - u * 512)
                L.append((w_f1i, 0, 8, u * 512, ncol))
                L.append((w_f1i, 0, 8, DFF + u * 512, ncol))
            for u in range(2):
                for kg, nk in enumerate((8, 8, 6)):
                    L.append((w_f1o, kg * 1024, nk, u * 512, 512))
            for col0 in (1024, 2048):
                for u in range(2):
                    L.append((w_in, 0, 8, col0 + u * 512, 512))
            for col0 in (0, 5120, 3072, 4096, 6144):
                for u in range(2):
                    L.append((w_in, 0, 8, col0 + u * 512, 512))
            for w in (w_pa, w_pb, w_out):
                for u in range(2):
                    L.append((w, 0, 8, u * 512, 512))
            for u in range(6):
                ncol = min(512, DFF - u * 512)
                L.append((w_f2i, 0, 8, u * 512, ncol))
                L.append((w_f2i, 0, 8, DFF + u * 512, ncol))
            for u in range(2):
                for kg, nk in enumerate((8, 8, 6)):
                    L.append((w_f2o, kg * 1024, nk, u * 512, 512))
            return L

        NPRE = 22
        conv_i = [NPRE]

        def convert_next(n):
            units = p2_unit_list()
            for _ in range(n):
                if conv_i[0] >= len(units):
                    return
                idx = conv_i[0]
                conv_i[0] += 1
                (w, r0, nk, c0, ncol) = units[idx]

                def loader(si, b, w=w, r0=r0, nk=nk, c0=c0, ncol=ncol):
                    S.dma("pool", wsl[si][:, 0:nk, 0:ncol], w[r0:r0 + nk * 128, c0:c0 + ncol].rearrange("(k p) f -> p k f", p=128), writes=[b])
                si, b = WS.get(loader)
                S.dma("sp", wbf[idx, :, 0:nk * ncol].rearrange("p (k f) -> p k f", f=ncol), wsl[si][:, 0:nk, 0:ncol], reads=[b], writes=[Bwbf[idx]])

        def preconvert():
            for idx, (w, r0, nk, c0, ncol) in enumerate(p2_unit_list()):
                p2units[(id(w), r0, c0)] = idx
                if idx >= NPRE:
                    continue
                si = idx % NW
                S.dma("pool", wsl[si][:, 0:nk, 0:ncol], w[r0:r0 + nk * 128, c0:c0 + ncol].rearrange("(k p) f -> p k f", p=128), writes=[WS.b[si]])
                S.dma("sp", wbf[idx, :, 0:nk * ncol].rearrange("p (k f) -> p k f", f=ncol), wsl[si][:, 0:nk, 0:ncol], reads=[WS.b[si]], writes=[Bwbf[idx]])

        def wunit(w, r0, nk, c0, ncol):
            if True:
                idx = p2units[(id(w), r0, c0)]

                def loader(si, b):
                    S.dma("sp", wsl[si][:, 0:nk, 0:ncol], wbf[idx, :, 0:nk * ncol].rearrange("p (k f) -> p k f", f=ncol), reads=[Bwbf[idx]], writes=[b])
            else:
                def loader(si, b):
                    S.dma("pool", wsl[si][:, 0:nk, 0:ncol], w[r0:r0 + nk * 128, c0:c0 + ncol].rearrange("(k p) f -> p k f", p=128), writes=[b])
            return WS.get(loader)

        def linear_tm(blocks, specs, evac):
            for u in range(2):
                slots = [wunit(w, 0, 8, c0 + u * 512, 512) for (_, _, w, c0) in specs]
                for tb, nt in enumerate(blocks):
                    banks = []
                    for (xT, BxT, _, _), (si, wb) in zip(specs, slots):
                        b = acc_bank()
                        banks.append(b)
                        for k in range(8):
                            S.op("pe", lambda e, k=k, b=b, si=si, xT=xT, tb=tb, nt=nt: e.matmul(
                                ps[0:nt, b, :], lhsT=xT[:, k, tb * 128:tb * 128 + nt], rhs=wsl[si][:, k, :], start=(k == 0), stop=(k == 7)),
                                 reads=[BxT[tb], wb], writes=[Bps[b]])
                    evac(tb, nt, u, banks)

        def postnorm_add(blocks, gain_bc):
            nb = len(blocks)
            sv = ssq[:, 0:2 * nb].rearrange("p (b u) -> p b u", u=2)
            S.op("dve", lambda e: e.tensor_tensor(out=ssum[:, 0:nb], in0=sv[:, :, 0], in1=sv[:, :, 1], op=ALU.add), reads=[Bssq], writes=[Bssq])
            rstd_chain(nb, 1.0 / D, src=ssum)
            for tb, nt in enumerate(blocks):
                S.op("dve", lambda e, tb=tb, nt=nt: e.scalar_tensor_tensor(out=F1[0:nt, tb, :], in0=F1[0:nt, tb, :], scalar=rstd[0:nt, tb:tb + 1],
                                                                          in1=gain_bc[0:nt, :], op0=ALU.mult, op1=ALU.mult),
                     reads=[BF1[tb], Brstd, Bconst], writes=[BF1[tb]])
                S.op("dve", lambda e, tb=tb, nt=nt: e.tensor_tensor(out=H[0:nt, tb, :], in0=H[0:nt, tb, :], in1=F1[0:nt, tb, :], op=ALU.add),
                     reads=[BH[tb], BF1[tb]], writes=[BH[tb]])

        def evac_raw_ss(tb, nt, u, banks):
            b = banks[0]
            S.op("act", lambda e: e.activation(out=junk[0:nt, 0:512], in_=ps[0:nt, b, :], func=AF.Square, accum_out=ssq[0:nt, tb * 2 + u:tb * 2 + u + 1]),
                 reads=[Bps[b]], writes=[Bssq, Bjunk])
            dstT = H if _os.environ.get("KF1") == "H" else F1
            S.op("dve", lambda e: e.tensor_copy(out=dstT[0:nt, tb, u * 512:(u + 1) * 512], in_=ps[0:nt, b, :]), reads=[Bps[b]], writes=[BF1[tb]])

        def ffn(blocks, w_i, w_o, gain_bc):
            Tn = (len(blocks) - 1) * 128 + blocks[-1]
            for u in range(6):
                ncol = min(512, DFF - u * 512)
                sa, ba = wunit(w_i, 0, 8, u * 512, ncol)
                sb, bb = wunit(w_i, 0, 8, DFF + u * 512, ncol)
                for jj in range(ncol // 128):
                    j = u * 4 + jj
                    pa_, pb_ = gen_bank(), gen_bank()
                    for (pbk, si, wb) in ((pa_, sa, ba), (pb_, sb, bb)):
                        for k in range(8):
                            S.op("pe", lambda e, k=k, pbk=pbk, si=si: e.matmul(ps[:, pbk, 0:Tn], lhsT=wsl[si][:, k, jj * 128:(jj + 1) * 128],
                                                                              rhs=fmA[:, k, 0:Tn], start=(k == 0), stop=(k == 7)),
                                 reads=BfmA[0:len(blocks)] + [wb], writes=[Bps[pbk]])
                    si2 = scr_next()
                    S.op("act", lambda e: e.activation(out=scr[si2][:, 0:Tn], in_=ps[:, pa_, 0:Tn], func=AF.Silu), reads=[Bps[pa_]], writes=[Bscr[si2]])
                    S.op("dve", lambda e: e.tensor_tensor(out=gT[:, j, 0:Tn], in0=scr[si2][:, 0:Tn], in1=ps[:, pb_, 0:Tn], op=ALU.mult),
                         reads=[Bscr[si2], Bps[pb_]], writes=[BgT[j]])
            if KSUB <= 2:
                return
            for u in range(2):
                bks = [acc_bank() for _ in blocks]
                for kg, nk in enumerate((8, 8, 6)):
                    si, wb = wunit(w_o, kg * 1024, nk, u * 512, 512)
                    for tb, nt in enumerate(blocks):
                        b = bks[tb]
                        for k in range(nk):
                            j = kg * 8 + k
                            S.op("pe", lambda e, k=k, j=j, b=b, si=si, tb=tb, nt=nt: e.matmul(
                                ps[0:nt, b, :], lhsT=gT[:, j, tb * 128:tb * 128 + nt], rhs=wsl[si][:, k, :], start=(j == 0), stop=(j == 21)),
                                 reads=[BgT[j], wb], writes=[Bps[b]])
                if KSUB <= 2.3:
                    continue
                for tb, nt in enumerate(blocks):
                    evac_raw_ss(tb, nt, u, [bks[tb]])
            if KSUB <= 2.6:
                return
            postnorm_add(blocks, gain_bc)

        def rope_inplace(si, nt, tb):
            v = scr[si][0:nt, 0:512].rearrange("p (g d) -> p g d", d=64)
            c = cosT[0:nt, tb, :].rearrange("p (g d) -> p g d", d=8)
            s_ = sinT[0:nt, tb, :].rearrange("p (g d) -> p g d", d=8)
            t = [rtmp[0:nt, i, :].rearrange("p (g d) -> p g d", d=8) for i in range(4)]
            x1, x2 = v[:, :, 0:8], v[:, :, 8:16]
            R = [Bscr[si], Brope]
            S.op("dve", lambda e: e.tensor_tensor(out=t[0], in0=x1, in1=c, op=ALU.mult), reads=R, writes=[Brtmp])
            S.op("dve", lambda e: e.tensor_tensor(out=t[1], in0=x2, in1=s_, op=ALU.mult), reads=R, writes=[Brtmp])
            S.op("dve", lambda e: e.tensor_tensor(out=t[2], in0=x2, in1=c, op=ALU.mult), reads=R, writes=[Brtmp])
            S.op("dve", lambda e: e.tensor_tensor(out=t[3], in0=x1, in1=s_, op=ALU.mult), reads=R, writes=[Brtmp])
            S.op("dve", lambda e: e.tensor_tensor(out=x1, in0=t[0], in1=t[1], op=ALU.subtract), reads=[Brtmp], writes=[Bscr[si]])
            S.op("dve", lambda e: e.tensor_tensor(out=x2, in0=t[2], in1=t[3], op=ALU.add), reads=[Brtmp], writes=[Bscr[si]])

        def load_rope(row0, nrows, nblk):
            if nblk > 1:
                S.dma("sp", cosT[:, 0:nblk, :], cosE[row0:row0 + nblk * 128, :].rearrange("(b p) f -> p b f", p=128), writes=[Brope])
                S.dma("sp", sinT[:, 0:nblk, :], sinE[row0:row0 + nblk * 128, :].rearrange("(b p) f -> p b f", p=128), writes=[Brope])
            else:
                S.dma("sp", cosT[0:nrows, 0, :], cosE[row0:row0 + nrows, :], writes=[Brope])
                S.dma("sp", sinT[0:nrows, 0, :], sinE[row0:row0 + nrows, :], writes=[Brope])

        def proj_rope_T(blocks, col0, dst, Bdst, out_fn=None, dst2=None, Bdst2=None):
            def evac(tb, nt, u, banks):
                b = banks[0]
                si = scr_next()
                S.op("act", lambda e: e.activation(out=scr[si][0:nt, 0:512], in_=ps[0:nt, b, :], func=AF.Copy), reads=[Bps[b]], writes=[Bscr[si]])
                rope_inplace(si, nt, tb)
                if out_fn is not None:
                    out_fn(si, tb, nt, u)
                bb = gen_bank()
                pv = ps[:, bb, :].rearrange("p (a c) -> p a c", c=128)
                for i in range(4):
                    S.op("pe", lambda e, i=i: e.transpose(pv[:, i, 0:nt], scr[si][0:nt, i * 128:(i + 1) * 128], identf[0:nt, 0:nt]),
                         reads=[Bscr[si], Bconst], writes=[Bps[bb]])
                if dst2 is None:
                    S.op("dve", lambda e: e.tensor_copy(out=dst[:, u * 4:u * 4 + 4, tb * 128:tb * 128 + nt], in_=pv[:, :, 0:nt]),
                         reads=[Bps[bb]], writes=[Bdst[tb]])
                else:
                    S.op("dve", lambda e: e.tensor_copy(out=dst[0:64, u * 4:u * 4 + 4, tb * 128:tb * 128 + nt], in_=pv[0:64, :, 0:nt]),
                         reads=[Bps[bb]], writes=[Bdst[tb]])
                    S.op("dve", lambda e: e.tensor_copy(out=dst2[64:128, u * 4:u * 4 + 4, tb * 128:tb * 128 + nt], in_=pv[64:128, :, 0:nt]),
                         reads=[Bps[bb]], writes=[Bdst2[tb]])
            linear_tm(blocks, [(fmA, BfmA, w_in, col0)], evac)

        def phase1(t):
            in_phase2[0] = False
            sample = (t == 32)
            own = (t < 4)
            slot, tt = t // 4, t % 4
            blocks = [48] if sample else [128] * 4
            nb = len(blocks)
            def load_x(t2):
                if t2 == 32:
                    S.op("dve", lambda e: e.memset(H[0:48, 0, :], 0.0), writes=[BH[0]])
                    for s0 in range(2):
                        S.dma("sp", H[s0 * 32:s0 * 32 + 16, 0, :], xs[s0], writes=[BH[0]])
                else:
                    S.dma("sp", H[:, :, :], xp[t2 * 512:(t2 + 1) * 512, :].rearrange("(b p) f -> p b f", p=128), writes=BH)
            if t == 0:
                load_x(0)
            if sample:
                load_rope(NCORE * NLOC, 48, 1)
            else:
                load_rope(t * 512, 128, 4)
            prenorm(blocks, gpre["f1pre"], fmA, BfmA)
            if KSUB <= 1:
                return
            ffn(blocks, w_f1i, w_f1o, gpost["f1post"])
            if KSUB <= 3:
                return
            if sample:
                S.dma("sp", h_scr[NLOC:NLOC + 48, :], H[0:48, 0, :], reads=[BH[0]], writes=[Bhscr[4]])
            elif own:
                S.dma("sp", h_scr[t * 512:(t + 1) * 512, :].rearrange("(b p) f -> p b f", p=128), H[:, :, :], reads=BH, writes=[Bhscr[t]])
            prenorm(blocks, gpre["mixpre"], fmA, BfmA)
            if t + 1 <= 32 and STAGE >= 2:
                load_x(t + 1)

            def k_out(si, tb, nt, u):
                if sample:
                    for s0 in range(2):
                        S.dma("sp", nk_s[s0, :, u * 512:(u + 1) * 512], scr[si][s0 * 32:s0 * 32 + 16, 0:512], reads=[Bscr[si]], writes=[])
                elif own:
                    r0 = t * 512 + tb * 128
                    S.dma("sp", nk_p[r0:r0 + 128, u * 512:(u + 1) * 512], scr[si][:, 0:512], reads=[Bscr[si]], writes=[])
            proj_rope_T(blocks, 1024, fmB, BfmB, out_fn=k_out)
            if sample:
                S.op("dve", lambda e: e.tensor_copy(out=ktS[:, :, 0:48], in_=fmB[:, :, 0:48]), reads=[BfmB[0]], writes=[BktS])
            else:
                S.dma("sp", kt_all[slot * D:(slot + 1) * D, tt * 512:(tt + 1) * 512].rearrange("(h d) n -> d h n", d=128), fmB[:, :, :], reads=BfmB, writes=[Bkt[t]])

            def v_evac(tb, nt, u, banks):
                b = banks[0]
                si = scr_next()
                S.op("act", lambda e: e.activation(out=scr[si][0:nt, 0:512], in_=ps[0:nt, b, :], func=AF.Copy), reads=[Bps[b]], writes=[Bscr[si]])
                if sample:
                    for i in range(2):
                        S.op("dve", lambda e, i=i: e.tensor_copy(out=vSs[i][i * 32:i * 32 + 16, u * 4:u * 4 + 4, 0:128],
                                                               in_=ps[i * 32:i * 32 + 16, b, :].rearrange("p (h f) -> p h f", f=128)),
                             reads=[Bps[b]], writes=[BvS])
                    for s0 in range(2):
                        S.dma("sp", nv_s[s0, :, u * 512:(u + 1) * 512], scr[si][s0 * 32:s0 * 32 + 16, 0:512], reads=[Bscr[si]], writes=[])
                else:
                    S.op("dve", lambda e: e.tensor_copy(out=tmC[:, tb, u * 512:(u + 1) * 512], in_=ps[:, b, :]), reads=[Bps[b]], writes=[BtmC[tb]])
                    if own:
                        r0 = t * 512 + tb * 128
                        S.dma("sp", nv_p[r0:r0 + 128, u * 512:(u + 1) * 512], scr[si][:, 0:512], reads=[Bscr[si]], writes=[])
            linear_tm(blocks, [(fmA, BfmA, w_in, 2048)], v_evac)
            if not sample:
                S.dma("sp", v_all[t * 512:(t + 1) * 512, :].rearrange("(b p) f -> p b f", p=128), tmC[:, :, :], reads=BtmC, writes=[Bvt[t]])
            convert_next(2)

        def gather():
            if S.dry:
                return
            for (src, dst, Bs, Bd, nm) in ((kt_in, kt_all, Bktin, Bktall, "k"), (v_in, v_all, Bvin, Bvall, "v")):
                S._deps("pool", [Bs], [Bd])
                cc = nc.gpsimd.collective_compute("AllGather", ALU.bypass, replica_groups=[list(range(NCORE))], ins=[src.opt()], outs=[dst.opt()])
                sem = st.enter_context(nc.semaphore("cc_" + nm))
                cc.then_inc(sem)
                ev = Ev("cc")
                ev.key = "cc" + nm
                ev.sem = sem
                ev.val = 1
                Bd.w = ev
                Bd.r = []

        def attn_finalize(acc_of, rows, qbs, h, dstF, BdstF):
            p0, p1 = rows
            nq = len(qbs)
            for i, qb in enumerate(qbs):
                for s in range(2):
                    b, off = acc_of(s, qb)
                    S.op("dve", lambda e, b=b, off=off, i=i, s=s: e.reciprocal(out=asm[p0:p1, s * 4 + i:s * 4 + i + 1], in_=ps[p0:p1, b, off + 128:off + 129]),
                         reads=[Bps[b]], writes=[Basm])
            S.op("dve", lambda e: e.tensor_scalar(out=asm[p0:p1, 4:4 + nq], in0=asm[p0:p1, 4:4 + nq], scalar1=lams[p0:p1, 2:3], scalar2=None, op0=ALU.mult),
                 reads=[Basm, Blam], writes=[Basm])
            for i, qb in enumerate(qbs):
                b0, o0 = acc_of(0, qb)
                b1, o1 = acc_of(1, qb)
                S.op("dve", lambda e, i=i, b1=b1, o1=o1: e.tensor_scalar(out=otmp2[p0:p1, i, :], in0=ps[p0:p1, b1, o1:o1 + 128], scalar1=asm[p0:p1, 4 + i:5 + i],
                                                                         scalar2=None, op0=ALU.mult),
                     reads=[Bps[b1], Basm], writes=[Botmp2])
                S.op("dve", lambda e, i=i, b0=b0, o0=o0: e.scalar_tensor_tensor(out=otmp[p0:p1, i, :], in0=ps[p0:p1, b0, o0:o0 + 128], scalar=asm[p0:p1, i:i + 1],
                                                                                in1=otmp2[p0:p1, i, :], op0=ALU.mult, op1=ALU.subtract),
                     reads=[Bps[b0], Basm, Botmp2], writes=[Botmp])
                S.op("act", lambda e, i=i: e.activation(out=otmp2[p0:p1, i, :], in_=otmp[p0:p1, i, :], func=AF.Square, accum_out=ssq[p0:p1, i:i + 1]),
                     reads=[Botmp], writes=[Botmp2, Bssq])
            rstd_chain(nq, 1.0 / 128)
            for i, qb in enumerate(qbs):
                S.op("dve", lambda e, i=i, qb=qb: e.scalar_tensor_tensor(out=dstF[p0:p1, qb, h * 128:(h + 1) * 128], in0=otmp[p0:p1, i, :],
                                                                        scalar=rstd[p0:p1, i:i + 1], in1=gsub[p0:p1, :], op0=ALU.mult, op1=ALU.mult),
                     reads=[Botmp, Brstd, Bconst], writes=[BdstF[qb]])

        def zero_acc(rows, banks_cols):
            for b, ncol in banks_cols:
                S.op("pe", lambda e, b=b, ncol=ncol: e.matmul(ps[0:rows, b, 0:ncol], lhsT=zerob[:, 0:rows], rhs=zerob[:, 0:ncol], start=True, stop=False,
                                                             skip_group_check=True), reads=[Bconst], writes=[Bps[b]])

        def attention_prompt(g):
            nb = 4 * g + 4
            nk = nb * 128

            def acc_of(s, qb):
                i = s * 4 + qb
                return i // 3, (i % 3) * 129

            for h in range(8):
                zero_acc(128, [(0, 387), (1, 387), (2, 258)])
                tiles = [(cp, rp) for cp in range(NCORE) for rp in range(nb)]
                kslot, vslot = {}, {}

                def emit_qk(idx):
                    cp, rp = tiles[idx]
                    if rp == 0:
                        def kl(si, b, cp=cp, h=h):
                            S.dma("sp", ktsl[si][:, 0:nk], kt_all[cp * D + h * 128:cp * D + (h + 1) * 128, 0:nk], reads=Bkt, writes=[b])
                        kslot[cp] = KS.get(kl)
                    ki, kb = kslot[cp]
                    a = rp - 4 * g
                    q0 = max(a, 0) * 128
                    bp = gen_pair()
                    for s in range(2):
                        b = bp + s
                        kT = ktsl[ki][:, rp * 128:(rp + 1) * 128]
                        qs = (fmB if s == 0 else fmC)[:, h, :]
                        if a >= 0:
                            S.op("pe", lambda e: e.matmul(ps[:, b, q0:q0 + 128], lhsT=kT, rhs=qs[:, q0:q0 + 128], start=True, stop=False),
                                 reads=[kb] + BfmB + BfmC, writes=[Bps[b]])
                            S.op("pe", lambda e: e.matmul(ps[:, b, q0:q0 + 128], lhsT=identb[:], rhs=maskB[:, cp, :], start=False, stop=True),
                                 reads=[Bconst], writes=[Bps[b]])
                            if q0 + 128 < 512:
                                S.op("pe", lambda e: e.matmul(ps[:, b, q0 + 128:512], lhsT=kT, rhs=qs[:, q0 + 128:512], start=True, stop=True),
                                     reads=[kb] + BfmB + BfmC, writes=[Bps[b]])
                        else:
                            S.op("pe", lambda e: e.matmul(ps[:, b, 0:512], lhsT=kT, rhs=qs[:, 0:512], start=True, stop=True),
                                 reads=[kb] + BfmB + BfmC, writes=[Bps[b]])
                    return bp, q0, a

                def emit_exp_av(idx, bp, q0, a):
                    cp, rp = tiles[idx]
                    ei = state["et"]
                    state["et"] = (ei + 1) % NET
                    for s in range(2):
                        S.op("act", lambda e: e.activation(out=ETs[ei][:, s, q0:512], in_=ps[:, bp + s, q0:512], func=AF.Exp, scale=0.125),
                             reads=[Bps[bp + s]], writes=[BET[ei]])
                    if rp == 0:
                        def vl(si, b, cp=cp, h=h):
                            S.dma("sp", vsl[si][:, 0:nb, 0:128],
                                  v_all[cp * NLOC:cp * NLOC + nk, h * 128:(h + 1) * 128].rearrange("(r p) f -> p r f", p=128), reads=Bvt, writes=[b])
                        vslot[cp] = VS.get(vl)
                    vi, vb = vslot[cp]
                    for qb in range(max(a, 0), 4):
                        for s in range(2):
                            ab, off = acc_of(s, qb)
                            S.op("pe", lambda e: e.matmul(ps[:, ab, off:off + 129], lhsT=ETs[ei][:, s, qb * 128:(qb + 1) * 128], rhs=vsl[vi][:, rp, :],
                                                          start=False, stop=(cp == NCORE - 1 and rp == 4 * g + qb), skip_group_check=True),
                                 reads=[BET[ei], vb], writes=[Bps[ab]])

                cur = emit_qk(0)
                for idx in range(len(tiles)):
                    nxt = emit_qk(idx + 1) if idx + 1 < len(tiles) else None
                    emit_exp_av(idx, *cur)
                    cur = nxt
                attn_finalize(acc_of, (0, 128), [0, 1, 2, 3], h, F1, BF1)

        def attention_sample():
            def acc_of(s, st_):
                i = s * 2 + st_
                return i // 3, (i % 3) * 129
            for h in range(8):
                zero_acc(48, [(0, 387), (1, 129)])
                for st_ in range(2):
                    def kl(si, b):
                        pass
                    ki, kb = KS.get(kl)
                    for k8 in range(2):
                        sci = scr_next()
                        kc = scr[sci][:, :].rearrange("p (k f) -> p k f", f=128)
                        S.dma("sp", kc, ck[st_, k8 * 1024:(k8 + 1) * 1024, h * 128:(h + 1) * 128].rearrange("(k p) f -> p k f", p=128), writes=[Bscr[sci]])
                        for q4 in range(2):
                            bb = gen_bank()
                            pv = ps[:, bb, :].rearrange("p (a c) -> p a c", c=128)
                            for i in range(4):
                                S.op("pe", lambda e: e.transpose(pv[:, i, :], kc[:, q4 * 4 + i, :], identf[:]), reads=[Bscr[sci], Bconst], writes=[Bps[bb]])
                            c0 = (k8 * 2 + q4) * 512
                            S.op("dve", lambda e: e.tensor_copy(out=ktsl[ki][:, c0:c0 + 512], in_=ps[:, bb, :]), reads=[Bps[bb]], writes=[kb])

                    def vl(si, b):
                        pass
                    vi, vb = VS.get(vl)
                    for k8 in range(2):
                        sci = scr_next()
                        vc = scr[sci][:, :].rearrange("p (k f) -> p k f", f=128)
                        S.dma("sp", vc, cv[st_, k8 * 1024:(k8 + 1) * 1024, h * 128:(h + 1) * 128].rearrange("(k p) f -> p k f", p=128), writes=[Bscr[sci]])
                        S.op("dve", lambda e: e.tensor_copy(out=vsl[vi][:, k8 * 8:(k8 + 1) * 8, 0:128], in_=vc), reads=[Bscr[sci]], writes=[vb])
                    for rp in range(16):
                        bp = gen_pair()
                        for s in range(2):
                            S.op("pe", lambda e: e.matmul(ps[:, bp + s, 0:48], lhsT=ktsl[ki][:, rp * 128:(rp + 1) * 128],
                                                          rhs=(fmB if s == 0 else fmC)[:, h, 0:48], start=True, stop=True),
                                 reads=[kb, BfmB[0], BfmC[0]], writes=[Bps[bp + s]])
                        ei = state["et"]
                        state["et"] = (ei + 1) % NET
                        for s in range(2):
                            S.op("act", lambda e: e.activation(out=ETs[ei][:, s, 0:48], in_=ps[:, bp + s, 0:48], func=AF.Exp, scale=0.125),
                                 reads=[Bps[bp + s]], writes=[BET[ei]])
                        for s in range(2):
                            ab, off = acc_of(s, st_)
                            S.op("pe", lambda e: e.matmul(ps[0:48, ab, off:off + 129], lhsT=ETs[ei][:, s, 0:48], rhs=vsl[vi][:, rp, :],
                                                          start=False, stop=False, skip_group_check=True),
                                 reads=[BET[ei], vb], writes=[Bps[ab]])
                bp = gen_pair()
                for s in range(2):
                    S.op("pe", lambda e: e.matmul(ps[0:48, bp + s, 0:48], lhsT=ktS[:, h, 0:48], rhs=(fmB if s == 0 else fmC)[:, h, 0:48],
                                                  start=True, stop=True),
                         reads=[BktS, BfmB[0], BfmC[0]], writes=[Bps[bp + s]])
                ei = state["et"]
                state["et"] = (ei + 1) % NET
                for s in range(2):
                    S.op("act", lambda e: e.activation(out=ETs[ei][0:48, s, 0:48], in_=ps[0:48, bp + s, 0:48], func=AF.Exp, scale=0.125),
                         reads=[Bps[bp + s]], writes=[BET[ei]])
                for st_ in range(2):
                    p0 = st_ * 32
                    for s in range(2):
                        ab, off = acc_of(s, st_)
                        S.op("pe", lambda e: e.matmul(ps[0:48, ab, off:off + 129], lhsT=ETs[ei][:, s, 0:48], rhs=vSs[st_][:, h, :],
                                                      start=False, stop=True, skip_group_check=True),
                             reads=[BET[ei], BvS], writes=[Bps[ab]])
                for st_ in range(2):
                    attn_finalize(lambda s, qb, st_=st_: acc_of(s, st_), (st_ * 32, st_ * 32 + 16), [0], h, F1, BF1)

        def phase2(t):
            in_phase2[0] = True
            sample = (t == 4)
            blocks = [48] if sample else [128] * 4
            nb = len(blocks)
            if sample:
                S.dma("sp", H[0:48, 0, :], h_scr[NLOC:NLOC + 48, :], reads=[Bhscr[4]], writes=[BH[0]])
                load_rope(NCORE * NLOC, 48, 1)
                S.op("dve", lambda e: e.memset(F1[0:48, 0, :], 0.0), writes=[BF1[0]])
            else:
                S.dma("sp", H[:, :, :], h_scr[t * 512:(t + 1) * 512, :].rearrange("(b p) f -> p b f", p=128), reads=[Bhscr[t]], writes=BH)
                load_rope(t * 512, 128, 4)
            prenorm(blocks, gpre["mixpre"], fmA, BfmA)
            S.op("dve", lambda e: e.memset(fmB[64:128, :, :], 0.0), writes=BfmB)
            S.op("dve", lambda e: e.memset(fmC[0:64, :, :], 0.0), writes=BfmC)
            proj_rope_T(blocks, 0, fmB, BfmB, dst2=fmC, Bdst2=BfmC)
            if KSUB2 <= 1:
                return
            if sample:
                attention_sample()
            else:
                attention_prompt(t)
            if KSUB2 <= 2:
                return
            for tb, nt in enumerate(blocks):
                transpose_to_fm(lambda k, tb=tb, nt=nt: F1[0:nt, tb, k * 128:(k + 1) * 128], nt, tb, fmC, BfmC, [BF1[tb]])

            def evac_a(tb, nt, u, banks):
                b1, b2 = banks
                si = scr_next()
                S.op("act", lambda e: e.activation(out=scr[si][0:nt, 0:512], in_=ps[0:nt, b2, :], func=AF.Sigmoid), reads=[Bps[b2]], writes=[Bscr[si]])
                S.op("dve", lambda e: e.tensor_tensor(out=F2[0:nt, tb, u * 512:(u + 1) * 512], in0=scr[si][0:nt, 0:512], in1=ps[0:nt, b1, :], op=ALU.mult),
                     reads=[Bscr[si], Bps[b1]], writes=[BF2[tb]])
            linear_tm(blocks, [(fmC, BfmC, w_pa, 0), (fmA, BfmA, w_in, 5120)], evac_a)
            if KSUB2 <= 3:
                return

            def evac_uv(tb, nt, u, banks):
                b1, b2 = banks
                S.op("act", lambda e: e.activation(out=F1[0:nt, tb, u * 512:(u + 1) * 512], in_=ps[0:nt, b1, :], func=AF.Gelu_apprx_tanh),
                     reads=[Bps[b1]], writes=[BF1[tb]])
                S.op("act", lambda e: e.activation(out=F3[0:nt, tb, u * 512:(u + 1) * 512], in_=ps[0:nt, b2, :], func=AF.Gelu_apprx_tanh),
                     reads=[Bps[b2]], writes=[BF3[tb]])
                S.op("act", lambda e: e.activation(out=junk[0:nt, 0:512], in_=F3[0:nt, tb, u * 512:(u + 1) * 512], func=AF.Square,
                                                   accum_out=ssq[0:nt, tb * 2 + u:tb * 2 + u + 1]),
                     reads=[BF3[tb]], writes=[Bssq, Bjunk])
            linear_tm(blocks, [(fmA, BfmA, w_in, 3072), (fmA, BfmA, w_in, 4096)], evac_uv)
            sv = ssq[:, 0:2 * nb].rearrange("p (b u) -> p b u", u=2)
            S.op("dve", lambda e: e.tensor_tensor(out=ssum[:, 0:nb], in0=sv[:, :, 0], in1=sv[:, :, 1], op=ALU.add), reads=[Bssq], writes=[Bssq])
            rstd_chain(nb, 1.0 / D, src=ssum)
            for tb, nt in enumerate(blocks):
                if sample:
                    si = scr_next()
                    S.op("dve", lambda e: e.scalar_tensor_tensor(out=scr[si][0:nt, :], in0=F3[0:nt, tb, :], scalar=rstd[0:nt, tb:tb + 1], in1=gpost["sgu"][0:nt, :],
                                                                 op0=ALU.mult, op1=ALU.mult), reads=[BF3[tb], Brstd, Bconst], writes=[Bscr[si]])
                    for s0 in range(2):
                        S.dma("sp", ng_s[s0, :, :], scr[si][s0 * 32:s0 * 32 + 16, :], reads=[Bscr[si]], writes=[])
                    S.op("dve", lambda e: e.tensor_copy(out=tmC[0:nt, tb, :], in_=scr[si][0:nt, :]), reads=[Bscr[si]], writes=[BtmC[tb]])
                else:
                    S.op("dve", lambda e, tb=tb, nt=nt: e.scalar_tensor_tensor(out=tmC[0:nt, tb, :], in0=F3[0:nt, tb, :], scalar=rstd[0:nt, tb:tb + 1],
                                                                              in1=gpost["sgu"][0:nt, :], op0=ALU.mult, op1=ALU.mult),
                         reads=[BF3[tb], Brstd, Bconst], writes=[BtmC[tb]])
                si = scr_next()
                for half in range(2):
                    b = gen_bank()
                    for gg in range(2):
                        g_ = half * 2 + gg
                        wT = wspS[0:48, g_, 0:48] if sample else wspT[:, g_, :]
                        S.op("pe", lambda e, b=b, gg=gg, g_=g_, wT=wT, tb=tb, nt=nt: e.matmul(ps[0:nt, b, gg * 256:(gg + 1) * 256], lhsT=wT,
                                                                                           rhs=tmC[0:nt, tb, g_ * 256:(g_ + 1) * 256], start=True, stop=True),
                             reads=[BtmC[tb], Bconst], writes=[Bps[b]])
                    for gg in range(2):
                        g_ = half * 2 + gg
                        bs = bspS if sample else bspT
                        S.op("dve", lambda e, b=b, gg=gg, g_=g_, bs=bs, tb=tb, nt=nt, si=si: e.scalar_tensor_tensor(
                            out=scr[si][0:nt, g_ * 256:(g_ + 1) * 256], in0=ps[0:nt, b, gg * 256:(gg + 1) * 256], scalar=bs[0:nt, g_:g_ + 1],
                            in1=F1[0:nt, tb, g_ * 256:(g_ + 1) * 256], op0=ALU.add, op1=ALU.mult),
                             reads=[Bps[b], BF1[tb], Bconst], writes=[Bscr[si]])
                transpose_to_fm(lambda k, si=si, nt=nt: scr[si][0:nt, k * 128:(k + 1) * 128], nt, tb, fmC, BfmC, [Bscr[si]])

            if KSUB2 <= 4:
                return
            def evac_b(tb, nt, u, banks):
                b1, b2 = banks
                si = scr_next()
                S.op("act", lambda e: e.activation(out=scr[si][0:nt, 0:512], in_=ps[0:nt, b2, :], func=AF.Sigmoid), reads=[Bps[b2]], writes=[Bscr[si]])
                S.op("dve", lambda e: e.tensor_tensor(out=scr[si][0:nt, 0:512], in0=scr[si][0:nt, 0:512], in1=ps[0:nt, b1, :], op=ALU.mult),
                     reads=[Bscr[si], Bps[b1]], writes=[Bscr[si]])
                S.op("dve", lambda e: e.tensor_tensor(out=F2[0:nt, tb, u * 512:(u + 1) * 512], in0=F2[0:nt, tb, u * 512:(u + 1) * 512], in1=scr[si][0:nt, 0:512], op=ALU.add),
                     reads=[Bscr[si], BF2[tb]], writes=[BF2[tb]])
            linear_tm(blocks, [(fmC, BfmC, w_pb, 0), (fmA, BfmA, w_in, 6144)], evac_b)
            for tb, nt in enumerate(blocks):
                transpose_to_fm(lambda k, tb=tb, nt=nt: F2[0:nt, tb, k * 128:(k + 1) * 128], nt, tb, fmB, BfmB, [BF2[tb]])
            linear_tm(blocks, [(fmB, BfmB, w_out, 0)], evac_raw_ss)
            postnorm_add(blocks, gpost["mixpost"])
            prenorm(blocks, gpre["f2pre"], fmA, BfmA)
            ffn(blocks, w_f2i, w_f2o, gpost["f2post"])
            if sample:
                for s0 in range(2):
                    S.dma("sp", y_s[s0, :, :], H[s0 * 32:s0 * 32 + 16, 0, :], reads=[BH[0]], writes=[])
            else:
                S.dma("sp", y_p[t * 512:(t + 1) * 512, :].rearrange("(b p) f -> p b f", p=128), H[:, :, :], reads=BH, writes=[])

        def program():
            for t in range(33):
                if STAGE >= 2 or (STAGE == 1 and t == 0):
                    phase1(t)
            for t in (4, 0, 1, 2, 3):
                if STAGE >= 6 or (STAGE == 4 and t == 4) or (STAGE == 5 and t in (4, 0)):
                    phase2(t)

        setup()
        preconvert()
        S.dry = True
        program()
        S.dry = False
        for s_ in (WS, KS, VS):
            s_.reset()
        for k in state:
            state[k] = 0
        conv_i[0] = NPRE
        program()
        S.finish([Bout])
        print("instructions:", S.ninst, "sbuf remaining:", nc.sbuf_bytes_remaining)
    return nc


_NC = None


def _rope_tables(pos):
    inv_freq = (500000.0 ** (-np.arange(0, 16, 2, dtype=np.float32) / 16)).astype(np.float32)
    ang = pos.astype(np.float32)[:, None] * inv_freq[None, :]
    cos = np.cos(ang).astype(np.float32)
    sin = np.sin(ang).astype(np.float32)
    return np.tile(cos, (1, 8)), np.tile(sin, (1, 8))


def kernel(x_prompt, x_sample, cache_k, cache_v,
           ln_ffn1_pre, w_ffn1_in, w_ffn1_out, ln_ffn1_post,
           ln_mix_pre, w_in, lambda_q1, lambda_k1, lambda_q2, lambda_k2,
           ln_subln, ln_sgu, w_spatial, b_spatial, w_proj_a, w_proj_b, w_out, ln_mix_post,
           ln_ffn2_pre, w_ffn2_in, w_ffn2_out, ln_ffn2_post):
    global _NC
    f = lambda a: np.ascontiguousarray(np.asarray(a, dtype=np.float32))
    if _NC is None:
        _NC = build_nc()
    nc = _NC
    xp = f(x_prompt)[0].reshape(NBLK, NCORE, 128, D)
    xs = f(x_sample)
    ck = f(cache_k)[0].reshape(16, 2048, D)
    cv = f(cache_v)[0].reshape(16, 2048, D)
    fm = lambda g: np.ascontiguousarray(f(g).reshape(8, 128).T)
    common = {
        "w_f1i": f(w_ffn1_in)[0], "w_f1o": f(w_ffn1_out)[0], "w_in": f(w_in)[0], "w_pa": f(w_proj_a)[0], "w_pb": f(w_proj_b)[0],
        "w_out": f(w_out)[0], "w_f2i": f(w_ffn2_in)[0], "w_f2o": f(w_ffn2_out)[0],
        "g_f1pre": fm(ln_ffn1_pre), "g_mixpre": fm(ln_mix_pre), "g_f2pre": fm(ln_ffn2_pre),
        "g_f1post": f(ln_ffn1_post), "g_mixpost": f(ln_mix_post), "g_f2post": f(ln_ffn2_post), "g_sgu": f(ln_sgu),
        "g_sub": f(ln_subln),
        "lamv": np.ascontiguousarray(np.concatenate([f(lambda_q1), f(lambda_k1), f(lambda_q2), f(lambda_k2)], 1).reshape(1, 256)),
        "wsp": f(w_spatial)[0], "bsp": f(b_spatial)[0],
        "ident": np.eye(128, dtype=np.float32),
        "trilT": np.ascontiguousarray(np.tril(np.ones((128, 128), np.float32))),
    }
    in_maps = []
    for c in range(NCORE):
        order = [c] + [r for r in range(NCORE) if r != c]
        pos = np.concatenate([((np.arange(NBLK)[:, None] * NCORE + r) * 128 + np.arange(128)[None, :]).reshape(-1) for r in order])
        spos = np.zeros(48, np.int64)
        spos[0:16] = 2048 + np.arange(16)
        spos[32:48] = 2048 + np.arange(16)
        cosE, sinE = _rope_tables(np.concatenate([pos, spos]))
        mb = np.zeros((128, 8, 128), np.float32)
        for sl in range(8):
            j = order[sl]
            if j == c:
                mb[64:128, sl, 0:64] = NEG
            elif j > c:
                mb[:, sl, :] = NEG
        m = dict(common)
        m.update({
            "xp": np.ascontiguousarray(np.concatenate([xp[:, r].reshape(NLOC, D) for r in order], 0)),
            "xs": np.ascontiguousarray(xs[2 * c:2 * c + 2]),
            "ck": np.ascontiguousarray(ck[2 * c:2 * c + 2]),
            "cv": np.ascontiguousarray(cv[2 * c:2 * c + 2]),
            "cosE": cosE, "sinE": sinE,
            "maskB": mb.astype(ml_dtypes.bfloat16),
        })
        in_maps.append(m)
    res = run_bass_kernel_spmd(nc, in_maps, core_ids=list(range(NCORE)))
    R = res.results

    def gp(name):
        a = np.stack([np.asarray(R[c][name], dtype=np.float32).reshape(NBLK, 128, D) for c in range(NCORE)], axis=1)
        return a.reshape(1, NBLK * NCORE * 128, D)

    def gs(name):
        return np.concatenate([np.asarray(R[c][name], dtype=np.float32) for c in range(NCORE)], axis=0)

    y_prompt = gp("y_p")
    y_sample = gs("y_s")
    nkp = gp("nk_p").reshape(1, 1, 16384, 8, 128)
    nvp = gp("nv_p").reshape(1, 1, 16384, 8, 128)
    nks = gs("nk_s").reshape(1, 16, 16, 8, 128)
    nvs = gs("nv_s").reshape(1, 16, 16, 8, 128)
    ngs = gs("ng_s").reshape(1, 16, 16, D)
    return (y_prompt, y_sample, nkp, nvp, nks, nvs, ngs)
```
